# Optimizing a Trainium2 kernel written in Bass

```python
import math
import jax, jax.numpy as jnp
from jax import lax
import numpy as np

D_MODEL = 1024
BATCH = 4
SEQ = 4096
DEPTH = 4
DEC_BATCH = 16
DEC_SEQ = 2048
PAST_LEN = 128

N_EVEN = (DEPTH + 1) // 2
N_ODD = DEPTH // 2
D_A = D_MODEL // 2
HEAD_A = 128
H_A = D_A // HEAD_A
CHUNK_A = 32
D_B = D_MODEL // 2
HEAD_B = 128
H_B = D_B // HEAD_B
CHUNK_B = 128
ROPE_BASE = 10000.0
D_MIX_EVEN = D_A + D_B
D_IN_EVEN = 5 * D_A + 4 * D_B
HEAD_C = 64
H_C = D_MODEL // HEAD_C
LORA_W = 64
LORA_A = 64
LORA_G = 128
LNX_EPS = 64e-5
MEM_LEN = 256
H_CA = 4
HEAD_CA = D_MODEL // H_CA
D_CA = H_CA * HEAD_CA
N_EXPERTS = 16
D_EXPERT = 2048
CAP_FACTOR = 2
DN_ALPHA = (2.0 * DEPTH) ** 0.25
DN_BETA = (8.0 * DEPTH) ** -0.25
LN_EPS = 1e-5

kernel_name = "hybrid_hgrn2_retnet_rwkv7_ec_moe_encoder"


def _layer_norm(x, w, b):
    xf = x.astype(jnp.float32)
    mu = jnp.mean(xf, -1, keepdims=True)
    var = jnp.mean(jnp.square(xf - mu), -1, keepdims=True)
    return ((xf - mu) * lax.rsqrt(var + LN_EPS) * w + b).astype(x.dtype)


def _head_norm(o, gain, bias, eps, center):
    of = o.astype(jnp.float32)
    if center:
        of = of - jnp.mean(of, -1, keepdims=True)
    of = of * lax.rsqrt(jnp.mean(jnp.square(of), -1, keepdims=True) + eps)
    of = of.reshape(of.shape[:-2] + (-1,)) * gain
    if bias is not None:
        of = of + bias
    return of


def _rotary(t, pos):
    d = t.shape[-1]
    theta = 1.0 / jnp.power(ROPE_BASE, jnp.linspace(0.0, 1.0, d // 2, dtype=jnp.float32))
    ang = pos[:, None] * theta[None, :]
    cos = jnp.cos(ang)[None, :, None, :]
    sin = jnp.sin(ang)[None, :, None, :]
    tf = t.astype(jnp.float32)
    t1, t2 = tf[..., 0::2], tf[..., 1::2]
    out = jnp.stack([t1 * cos - t2 * sin, t2 * cos + t1 * sin], -1).reshape(t.shape)
    return out.astype(t.dtype)


def _bidir(fwd, bwd):
    return jnp.stack([fwd, jnp.flip(bwd, axis=1)])


def _merge(o):
    return o[0] + jnp.flip(o[1], axis=1)


def _to_chunks(t, L):
    N, B, T, H, d = t.shape
    return t.reshape(N, B, T // L, L, H, d).transpose(2, 0, 1, 4, 3, 5)


def _from_chunks(t):
    nC, N, B, H, L, d = t.shape
    return t.transpose(1, 2, 0, 4, 3, 5).reshape(N, B, nC * L, H, d)


def _hgrn2_chunkwise(q, k, v, log_f):
    N, B, T, H, dk = q.shape
    dv = v.shape[-1]
    causal = jnp.tri(CHUNK_A, dtype=bool)

    def step(S, inp):
        qc, kc, vc, lfc = [t.astype(jnp.float32) for t in inp]
        b = jnp.cumsum(lfc, axis=-2)
        diff = b[..., :, None, :] - b[..., None, :, :]
        decay = jnp.exp(jnp.where(causal[:, :, None], diff, -jnp.inf))
        scores = jnp.einsum('nbhtsk,nbhsk->nbhts', decay * qc[..., :, None, :], kc)
        o = jnp.einsum('nbhts,nbhsv->nbhtv', scores, vc) + \
            jnp.einsum('nbhtk,nbhkv->nbhtv', qc * jnp.exp(b), S)
        b_last = b[..., -1:, :]
        S = jnp.exp(b_last[..., 0, :])[..., None] * S + \
            jnp.einsum('nbhsk,nbhsv->nbhkv', kc * jnp.exp(b_last - b), vc)
        return S, o

    S0 = jnp.zeros((N, B, H, dk, dv), jnp.float32)
    xs = (_to_chunks(q, CHUNK_A), _to_chunks(k, CHUNK_A), _to_chunks(v, CHUNK_A), _to_chunks(log_f, CHUNK_A))
    _, o = lax.scan(step, S0, xs)
    return _from_chunks(o).astype(v.dtype)


def _retention_chunkwise(q, k, v, log_gamma, intra_mask):
    N, B, T, H, dk = q.shape
    dv = v.shape[-1]
    pos = jnp.arange(CHUNK_B, dtype=jnp.float32)
    lg = log_gamma.astype(jnp.float32)
    dist = jnp.maximum(pos[:, None] - pos[None, :], 0.0)
    d_intra = jnp.where(intra_mask[:, None], jnp.exp(lg[:, None, None] * dist), 0.0)
    q_decay = jnp.exp(lg[:, None] * (pos + 1.0))[:, :, None]
    k_decay = jnp.exp(lg[:, None] * (CHUNK_B - 1.0 - pos))[:, :, None]
    chunk_decay = jnp.exp(lg * CHUNK_B)[:, None, None]

    def step(R, inp):
        qc, kc, vc = [t.astype(jnp.float32) for t in inp]
        scores = jnp.einsum('nbhtk,nbhsk->nbhts', qc, kc) * d_intra[:, None]
        o = jnp.einsum('nbhts,nbhsv->nbhtv', scores, vc) + \
            jnp.einsum('nbhtk,nbhkv->nbhtv', qc * q_decay, R)
        R = chunk_decay * R + jnp.einsum('nbhsk,nbhsv->nbhkv', kc * k_decay, vc)
        return R, o

    R0 = jnp.zeros((N, B, H, dk, dv), jnp.float32)
    xs = (_to_chunks(q, CHUNK_B), _to_chunks(k, CHUNK_B), _to_chunks(v, CHUNK_B))
    _, o = lax.scan(step, R0, xs)
    return _from_chunks(o).astype(v.dtype)


def _even_mixer(x, w_in, lb, norm_a, norm_b, w_out):
    Bn, T, _ = x.shape
    cuts = [D_A * i for i in range(1, 6)] + [5 * D_A + D_B * j for j in range(1, 4)]
    a_q, a_i, a_ff, a_fb, a_g, b_q, b_k, b_v, b_g = jnp.split(x @ w_in, cuts, axis=-1)

    heads_a = lambda t: t.reshape(t.shape[:-1] + (H_A, HEAD_A))
    z = jnp.stack([a_ff, a_fb]).astype(jnp.float32)
    lbf = lb.astype(jnp.float32)[:, None, None, :]
    log_f = jnp.logaddexp(jnp.log(lbf), jnp.log1p(-lbf) + jax.nn.log_sigmoid(z))
    k_in = (1.0 - lbf) * jax.nn.sigmoid(-z)
    q_in = jax.nn.silu(a_q)
    o_a = _hgrn2_chunkwise(heads_a(_bidir(q_in, q_in)), heads_a(_bidir(k_in[0], k_in[1])),
                           heads_a(_bidir(a_i, a_i)), heads_a(_bidir(log_f[0], log_f[1])))
    o_a = _head_norm(_merge(o_a), norm_a, None, 1e-6, False) * jax.nn.silu(a_g)

    heads_b = lambda t: t.reshape(t.shape[:-1] + (H_B, HEAD_B))
    pos = jnp.arange(T, dtype=jnp.float32)
    qb = _rotary(heads_b(b_q), pos)
    kb = _rotary(heads_b(b_k), pos) * (HEAD_B ** -0.5)
    vb = heads_b(b_v)
    log_gamma = jnp.log1p(-jnp.power(2.0, -5.0 - jnp.arange(H_B, dtype=jnp.float32)))
    mask = jnp.stack([jnp.tri(CHUNK_B, dtype=bool), jnp.tri(CHUNK_B, k=-1, dtype=bool)])
    o_b = _retention_chunkwise(_bidir(qb, qb), _bidir(kb, kb), _bidir(vb, vb), log_gamma, mask)
    o_b = _head_norm(_merge(o_b), norm_b, None, 1e-6, True) * jax.nn.silu(b_g)

    y = jnp.concatenate([o_a, o_b.astype(o_a.dtype)], -1) @ w_out
    return y.astype(x.dtype)


def _odd_mixer(x, mu, w_rkv, w0, w1, w2, a0, a1, a2, g1, g2, k_k, k_a, r_k, lnx_w, lnx_b, w_out):
    Bn, T, D = x.shape
    zero = jnp.zeros_like(x[:, :1])
    x_prev = jnp.concatenate([zero, x[:, :-1]], 1)
    x_next = jnp.concatenate([x[:, 1:], zero], 1)
    xx = 0.5 * (x_prev + x_next) - x
    xm = x[None] + xx[None] * mu[:, None, None, :]
    rkv = jnp.einsum('pbtd,pde->pbte', xm[:3], w_rkv)
    r, k, v = rkv[0], rkv[1], rkv[2]
    xw, xa, xg = xm[3], xm[4], xm[5]
    lora_w = jnp.einsum('nbtl,nld->nbtd', jnp.tanh(jnp.einsum('btd,ndl->nbtl', xw, w1)), w2)
    w_log = -jax.nn.softplus(-(w0[:, None, None, :] + lora_w).astype(jnp.float32)) - 0.5
    decay = jnp.exp(-jnp.exp(w_log))
    a = jax.nn.sigmoid((a0[:, None, None, :] + jnp.einsum('nbtl,nld->nbtd', jnp.einsum('btd,ndl->nbtl', xa, a1), a2)).astype(jnp.float32))
    g = jax.nn.sigmoid(xg @ g1) @ g2

    heads = lambda t: t.reshape(t.shape[:-1] + (H_C, HEAD_C))
    kk = heads(k * k_k).astype(jnp.float32)
    kk = kk / jnp.maximum(jnp.sqrt(jnp.sum(jnp.square(kk), -1, keepdims=True)), 1e-12)
    a_h = heads(a)
    k_dir = heads(k).astype(jnp.float32)[None] * (1.0 + (a_h - 1.0) * heads(k_a))
    decay_h = heads(decay)
    r_h, v_h = heads(r), heads(v)
    tm = lambda t: jnp.moveaxis(t, 1, 0).astype(jnp.float32)
    r_s, v_s, kk_s = tm(r_h), tm(v_h), tm(kk)

    def run(n, reverse):
        def step(S, inp):
            r_t, w_t, k_t, v_t, kk_t, b_t = inp
            sa = jnp.einsum('bhvk,bhk->bhv', S, kk_t)
            S = S * w_t[:, :, None, :] - sa[..., None] * b_t[:, :, None, :] + v_t[..., None] * k_t[:, :, None, :]
            return S, jnp.einsum('bhvk,bhk->bhv', S, r_t)
        S0 = jnp.zeros((Bn, H_C, HEAD_C, HEAD_C), jnp.float32)
        xs = (r_s, tm(decay_h[n]), tm(k_dir[n]), v_s, kk_s, tm(kk * a_h[n]))
        _, ys = lax.scan(step, S0, xs, reverse=reverse)
        return ys

    o = jnp.moveaxis(run(0, False) + run(1, True), 0, 1)
    o = _head_norm(o, lnx_w, lnx_b, LNX_EPS, True)
    bonus = jnp.sum(r_h[None].astype(jnp.float32) * k_dir * heads(r_k), -1, keepdims=True) * v_h[None]
    bonus = (bonus[0] + bonus[1]).reshape(Bn, T, D)
    y = ((o + bonus) * g) @ w_out
    return y.astype(x.dtype)


def _cross_attn(x, mem, w_q, w_kv, w_out):
    Bn, T, _ = x.shape
    M = mem.shape[1]
    q = (x @ w_q).reshape(Bn, T, H_CA, HEAD_CA)
    kv = (mem @ w_kv).reshape(Bn, M, 2, H_CA, HEAD_CA)
    s = jnp.einsum('bthd,bmhd->bhtm', q, kv[:, :, 0]).astype(jnp.float32) * (HEAD_CA ** -0.5)
    p = jax.nn.softmax(s, axis=-1).astype(x.dtype)
    o = jnp.einsum('bhtm,bmhd->bthd', p, kv[:, :, 1]).reshape(Bn, T, D_CA)
    return (o @ w_out).astype(x.dtype)


def _expert_choice_ffn(x, w_router, w_in, w_out):
    Bn, T, D = x.shape
    n = Bn * T
    cap = (CAP_FACTOR * n) // N_EXPERTS
    xt = x.reshape(n, D)
    aff = jax.nn.softmax((xt @ w_router).astype(jnp.float32), axis=-1)
    gate, idx = lax.top_k(aff.T, cap)
    xe = xt[idx]
    h_gate, h_up = jnp.split(jnp.einsum('ecd,edf->ecf', xe, w_in), 2, axis=-1)
    ye = jnp.einsum('ecf,efd->ecd', jax.nn.silu(h_gate) * h_up, w_out) * gate[..., None].astype(x.dtype)
    y = jnp.zeros_like(xt).at[idx.reshape(-1)].add(ye.reshape(-1, D).astype(xt.dtype))
    return y.reshape(Bn, T, D)


def _trunk(x, mem, p):
    p_lb = jax.nn.softmax(p['ev_lb_logits'].astype(jnp.float32), axis=1)
    lb_all = jnp.clip(jnp.cumsum(p_lb, axis=1) - p_lb[:, :1], 0.0, 1.0)
    for layer in range(DEPTH):
        j = layer // 2
        if layer % 2 == 0:
            h = _even_mixer(x, p['ev_w_in'][j], lb_all[:, j], p['ev_norm_a'][j], p['ev_norm_b'][j], p['ev_w_out'][j])
        else:
            h = _odd_mixer(x, p['od_mu'][j], p['od_w_rkv'][j], p['od_w0'][j], p['od_w1'][j], p['od_w2'][j],
                           p['od_a0'][j], p['od_a1'][j], p['od_a2'][j], p['od_g1'][j], p['od_g2'][j],
                           p['od_k_k'][j], p['od_k_a'][j], p['od_r_k'][j], p['od_lnx_w'][j], p['od_lnx_b'][j],
                           p['od_w_out'][j])
        x = _layer_norm(DN_ALPHA * x + h, p['ln_w'][layer, 0], p['ln_b'][layer, 0])
        h = _cross_attn(x, mem, p['ca_w_q'][layer], p['ca_w_kv'][layer], p['ca_w_out'][layer])
        x = _layer_norm(DN_ALPHA * x + h, p['ln_w'][layer, 1], p['ln_b'][layer, 1])
        h = _expert_choice_ffn(x, p['moe_router'][layer], p['moe_w_in'][layer], p['moe_w_out'][layer])
        x = _layer_norm(DN_ALPHA * x + h, p['ln_w'][layer, 2], p['ln_b'][layer, 2])
    return x


def setup_inputs(seed: int = 0) -> dict:
    key = jax.random.key(seed)
    k = jax.random.split(key, 33)
    D = D_MODEL
    nrm = lambda i, shape, scale: jax.random.normal(k[i], shape, jnp.float32) * scale
    uni = lambda i, shape, lo, hi: jax.random.uniform(k[i], shape, jnp.float32, lo, hi)
    return {
        'x_prompt': nrm(0, (BATCH, SEQ, D), 1.0),
        'x_sample': nrm(1, (DEC_BATCH, DEC_SEQ, D), 1.0),
        'mem_prompt': nrm(2, (BATCH, MEM_LEN, D), 1.0),
        'mem_sample': nrm(3, (DEC_BATCH, MEM_LEN, D), 1.0),
        'ev_w_in': nrm(4, (N_EVEN, D, D_IN_EVEN), D ** -0.5),
        'ev_lb_logits': nrm(5, (2, N_EVEN, D_A), 0.5),
        'ev_norm_a': 1.0 + nrm(6, (N_EVEN, D_A), 0.02),
        'ev_norm_b': 1.0 + nrm(7, (N_EVEN, D_B), 0.02),
        'ev_w_out': nrm(8, (N_EVEN, D_MIX_EVEN, D), D_MIX_EVEN ** -0.5 * DN_BETA),
        'od_mu': uni(9, (N_ODD, 6, D), 0.0, 1.0),
        'od_w_rkv': nrm(10, (N_ODD, 3, D, D), D ** -0.5),
        'od_w0': uni(11, (N_ODD, 2, D), -4.0, -0.5),
        'od_w1': nrm(12, (N_ODD, 2, D, LORA_W), D ** -0.5),
        'od_w2': nrm(13, (N_ODD, 2, LORA_W, D), 0.1 * LORA_W ** -0.5),
        'od_a0': nrm(14, (N_ODD, 2, D), 0.1),
        'od_a1': nrm(15, (N_ODD, 2, D, LORA_A), D ** -0.5),
        'od_a2': nrm(16, (N_ODD, 2, LORA_A, D), 0.1 * LORA_A ** -0.5),
        'od_g1': nrm(17, (N_ODD, D, LORA_G), D ** -0.5),
        'od_g2': nrm(18, (N_ODD, LORA_G, D), LORA_G ** -0.5),
        'od_k_k': 0.85 + nrm(19, (N_ODD, D), 0.02),
        'od_k_a': 1.0 + nrm(20, (N_ODD, D), 0.02),
        'od_r_k': nrm(21, (N_ODD, D), 0.1),
        'od_lnx_w': 1.0 + nrm(22, (N_ODD, D), 0.02),
        'od_lnx_b': nrm(23, (N_ODD, D), 0.02),
        'od_w_out': nrm(24, (N_ODD, D, D), D ** -0.5 * DN_BETA),
        'ca_w_q': nrm(25, (DEPTH, D, D_CA), D ** -0.5),
        'ca_w_kv': nrm(26, (DEPTH, D, 2 * D_CA), D ** -0.5),
        'ca_w_out': nrm(27, (DEPTH, D_CA, D), D_CA ** -0.5 * DN_BETA),
        'moe_router': nrm(28, (DEPTH, D, N_EXPERTS), D ** -0.5),
        'moe_w_in': nrm(29, (DEPTH, N_EXPERTS, D, 2 * D_EXPERT), D ** -0.5),
        'moe_w_out': nrm(30, (DEPTH, N_EXPERTS, D_EXPERT, D), D_EXPERT ** -0.5 * DN_BETA),
        'ln_w': 1.0 + nrm(31, (DEPTH, 3, D), 0.02),
        'ln_b': nrm(32, (DEPTH, 3, D), 0.02),
    }


def reference(x_prompt, x_sample, mem_prompt, mem_sample, ev_w_in, ev_lb_logits, ev_norm_a, ev_norm_b, ev_w_out,
              od_mu, od_w_rkv, od_w0, od_w1, od_w2, od_a0, od_a1, od_a2, od_g1, od_g2, od_k_k, od_k_a, od_r_k,
              od_lnx_w, od_lnx_b, od_w_out, ca_w_q, ca_w_kv, ca_w_out, moe_router, moe_w_in, moe_w_out, ln_w, ln_b):
    params = dict(ev_w_in=ev_w_in, ev_lb_logits=ev_lb_logits, ev_norm_a=ev_norm_a, ev_norm_b=ev_norm_b,
                  ev_w_out=ev_w_out, od_mu=od_mu, od_w_rkv=od_w_rkv, od_w0=od_w0, od_w1=od_w1, od_w2=od_w2,
                  od_a0=od_a0, od_a1=od_a1, od_a2=od_a2, od_g1=od_g1, od_g2=od_g2, od_k_k=od_k_k,
                  od_k_a=od_k_a, od_r_k=od_r_k, od_lnx_w=od_lnx_w, od_lnx_b=od_lnx_b, od_w_out=od_w_out,
                  ca_w_q=ca_w_q, ca_w_kv=ca_w_kv, ca_w_out=ca_w_out, moe_router=moe_router,
                  moe_w_in=moe_w_in, moe_w_out=moe_w_out, ln_w=ln_w, ln_b=ln_b)
    y_prompt = _trunk(x_prompt, mem_prompt, params)
    y_sample = _trunk(x_sample, mem_sample, params)
    return (y_prompt, y_sample)
```

```python
import numpy as np
import concourse.bass as bass
import concourse.mybir as mybir
from concourse.bass_utils import run_bass_kernel_spmd

F32 = mybir.dt.float32
BF16 = mybir.dt.bfloat16
AF = mybir.ActivationFunctionType
ALU = mybir.AluOpType
AX = mybir.AxisListType
PE, ACT, DVE, POOL, SP = "tensor", "scalar", "vector", "gpsimd", "sync"

NCORES = 8
D = 1024
T = 6144
TU = 2048
NU = 3
NTILE = T // 128
DEPTH = 4
DN_ALPHA = (2.0 * DEPTH) ** 0.25
LN_EPS = 1e-5
NE = 16
DEXP = 2048
CAP_P = 2048
CAP_S = 4096
MEM = 256
SEM_CAP = 30000
DBG_TILES = None
F32T_ENG = None
F32T_SKIP = False
F32T_2D = False
SBW = 51200


class Tok:
    __slots__ = ("w", "r")

    def __init__(self):
        self.w = None
        self.r = {}


class FW:
    def __init__(self, n_dma_sems=16):
        self.nc = bass.Bass("TRN2", target_bir_lowering=False)
        self.q = {e: [] for e in (PE, ACT, DVE, POOL, SP)}
        self.cnt = {e: 0 for e in self.q}
        self.waited = {e: {} for e in self.q}
        self.n_dma_sems = n_dma_sems
        self.dma_val = {}
        self.dma_rr = {e: 0 for e in self.q}
        self.dma_ep = {}
        self.log = []
        self.ctx = []
        self.sems = {}
        self.ninstr = 0

    def enter(self, cm):
        v = cm.__enter__()
        self.ctx.append(cm)
        return v

    def sem(self, key):
        if key not in self.sems:
            nm = "s_" + "_".join(str(k) for k in key)
            self.sems[key] = self.enter(self.nc.semaphore(nm))
        return self.sems[key]

    def _wait(self, eng, key, val):
        if key[0] == eng and eng == PE and len(key) == 2:
            return
        if self.waited[eng].get(key, 0) >= val:
            return
        self.waited[eng][key] = val
        s = self.sem(key)
        self.q[eng].append(lambda e, s=s, val=val: e.wait_ge(s, val))
        self.log.append((eng, 'wait', key, val))

    def _deps(self, eng, reads, writes):
        for t in reads:
            if t.w is not None:
                self._wait(eng, *t.w)
        for t in writes:
            if t.w is not None:
                self._wait(eng, *t.w)
            for k, v in t.r.items():
                self._wait(eng, k, v)

    def _mark(self, key, val, reads, writes):
        for t in reads:
            t.r[key] = val
        for t in writes:
            t.w = (key, val)
            t.r = {}

    def _ekey(self, eng):
        c = self.cnt[eng]
        return (eng, (c - 1) // SEM_CAP), (c - 1) % SEM_CAP + 1

    def op(self, eng, fn, reads=(), writes=()):
        self._deps(eng, reads, writes)
        self.cnt[eng] += 1
        key, val = self._ekey(eng)
        s = self.sem(key)
        self.q[eng].append(lambda e, fn=fn, s=s: fn(e).then_inc(s, 1))
        self.log.append((eng, 'op', key, val))
        self._mark(key, val, reads, writes)
        self.ninstr += 1

    def dma(self, out, in_, reads=(), writes=(), q=SP, fn=None, **kw):
        self._deps(q, reads, writes)
        i = self.dma_rr[q] % self.n_dma_sems
        self.dma_rr[q] += 1
        ep = self.dma_ep.get((q, i), 0)
        key = (q, i, ep)
        prev = self.dma_val.get(key, 0)
        if prev + 16 > SEM_CAP:
            self._wait(q, key, prev)
            ep += 1
            self.dma_ep[(q, i)] = ep
            key = (q, i, ep)
            prev = 0
        if prev:
            self._wait(q, key, prev)
        val = prev + 16
        self.dma_val[key] = val
        s = self.sem(key)
        if fn is None:
            self.q[q].append(lambda e, s=s, out=out, in_=in_, kw=kw:
                             e.dma_start(out=out, in_=in_, **kw).then_inc(s, 16))
        else:
            self.q[q].append(lambda e, s=s, fn=fn: fn(e).then_inc(s, 16))
        self.log.append((q, 'dma', key, val))
        self._mark(key, val, reads, writes)
        self.ninstr += 1

    def barrier(self):
        for eng in self.q:
            for e2 in self.q:
                if e2 != eng and self.cnt[e2]:
                    key, val = self._ekey(e2)
                    self._wait(eng, key, val)
            for key, val in self.dma_val.items():
                self._wait(eng, key, val)

    def finish(self):
        self.barrier()
        nc = self.nc
        with nc.Block() as block:
            for name in (PE, ACT, DVE, POOL, SP):
                lst = self.q[name]

                def body(e, lst=lst):
                    for f in lst:
                        f(e)
                getattr(block, name)(body)
        for cm in reversed(self.ctx):
            cm.__exit__(None, None, None)
        self.ctx = []
        return nc


def _prod(s):
    r = 1
    for v in s:
        r *= v
    return r


class KB(FW):
    def __init__(self):
        super().__init__()
        nc = self.nc
        self.big = self.enter(nc.sbuf_tensor("big", [128, SBW], F32))
        self.off = 0
        self.base = 0
        self.banks = [(self.enter(nc.psum_tensor("ps%d" % i, [128, 512], F32))[:, :], Tok()) for i in range(8)]
        self.bank_i = 0
        self.ev_i = 0

    def tile(self, shape, dt=F32):
        parts = shape[0]
        n = _prod(shape[1:])
        words = n if dt == F32 else (n + 1) // 2
        words = (words + 1) // 2 * 2
        assert self.off + words <= SBW, ("SBUF arena overflow", self.off, words)
        ap = self.big[:, self.off:self.off + words]
        self.off += words
        if dt != F32:
            ap = ap.bitcast(dt)
        ap = ap[:parts, :n]
        if len(shape) > 2:
            names = " ".join("d%d" % i for i in range(len(shape) - 1))
            kw = {"d%d" % i: shape[i + 1] for i in range(len(shape) - 2)}
            ap = ap.rearrange("p (%s) -> p %s" % (names, names), **kw)
        return ap, Tok()

    def ring(self, n, shape, dt=F32):
        return _Ring([self.tile(shape, dt) for _ in range(n)])

    def stage_begin(self):
        self.barrier()
        self.off = self.base

    def persist(self):
        self.base = self.off

    def ps(self):
        r = self.banks[self.bank_i % 8]
        self.bank_i += 1
        return r

    def ev(self):
        self.ev_i += 1
        return ACT if self.ev_i % 2 else DVE

    def copy(self, eng, out, in_, reads, writes, scale=None):
        if eng == ACT:
            if scale is None:
                self.op(ACT, lambda e: e.copy(out=out, in_=in_), reads, writes)
            else:
                self.op(ACT, lambda e: e.mul(out=out, in_=in_, mul=scale), reads, writes)
        else:
            if scale is None:
                self.op(eng, lambda e: e.tensor_copy(out=out, in_=in_), reads, writes)
            else:
                self.op(eng, lambda e: e.tensor_scalar(out=out, in0=in_, scalar1=scale, scalar2=None,
                                                       op0=ALU.mult), reads, writes)


class _Ring:
    def __init__(self, tiles):
        self.tiles = tiles
        self.i = 0

    def next(self):
        r = self.tiles[self.i % len(self.tiles)]
        self.i += 1
        return r


def _build_consts():
    lay = {}
    cols = []
    off = [0]

    def add(name, arr):
        arr = np.asarray(arr, np.float64)
        lay[name] = (off[0], arr.shape[1])
        off[0] += arr.shape[1]
        cols.append(arr)

    idx = np.arange(128)
    s_ = idx[:, None]
    t_ = idx[None, :]
    same = (s_ // 64) == (t_ // 64)
    add("ident", np.eye(128))
    add("ones", np.ones((128, 128)))
    add("tri_f", same & (s_ <= t_))
    add("tri_b", same & (s_ >= t_))
    add("blk", same)
    ind = np.zeros((128, 8))
    ind[:64, 0] = 1
    ind[64:, 1] = 1
    add("ind", ind)
    add("us", s_ < t_)
    add("ui", s_ <= t_)
    add("ls", s_ > t_)
    add("li", s_ >= t_)
    gam = 1.0 - 2.0 ** (-5.0 - np.arange(4))
    for d in range(2):
        for h in range(4):
            if d == 0:
                m = np.where(s_ <= t_, gam[h] ** np.maximum(t_ - s_, 0), 0.0)
            else:
                m = np.where(s_ > t_, gam[h] ** np.maximum(s_ - t_, 0), 0.0)
            add("rmask%d%d" % (d, h), m)
    for d in range(2):
        for h in range(4):
            row = gam[h] ** (idx + 1.0) if d == 0 else gam[h] ** (128.0 - idx)
            add("rqd%d%d" % (d, h), np.broadcast_to(row[None, :], (128, 128)))
    kd = np.zeros((128, 8))
    for d in range(2):
        for h in range(4):
            kd[:, d * 4 + h] = gam[h] ** (127.0 - idx) if d == 0 else gam[h] ** (idx * 1.0)
    add("rkd", kd)
    c = np.concatenate(cols, axis=1).astype(np.float32)
    return lay, c, gam


CONST_LAYOUT, _CONSTS, _GAM = _build_consts()
NCONST = _CONSTS.shape[1]
C_IDENT = CONST_LAYOUT["ident"][0]
C_ONES = CONST_LAYOUT["ones"][0]


def make_consts():
    return _CONSTS


def make_rope(c):
    pos = np.concatenate([np.arange(2048) + (2048 if (c < 4 and u == 1) else 0) for u in range(NU)]).astype(np.float32)
    theta = (1.0 / np.power(np.float32(10000.0), np.linspace(0.0, 1.0, 64, dtype=np.float32))).astype(np.float32)
    ang = pos[:, None] * theta[None, :]
    out = np.zeros((T, 192), np.float32)
    out[:, 0:128] = np.repeat(np.cos(ang), 2, axis=1)
    out[:, 128:192] = np.sin(ang)
    return out


def make_flags(c):
    fl = np.zeros((128, 16), np.float32)
    if c < 4:
        fl[:, 1] = 1
        fl[:, 3] = 1
        fl[:, 6] = 1
        fl[:, 7] = 1
    fl[:, 9:12] = 1 - fl[:, 6:9]
    return fl


def core_units(c):
    if c < 4:
        return [("p", c, 0), ("p", c, 1), ("s", c, 0)]
    b = 4 + 3 * (c - 4)
    return [("s", b, 0), ("s", b + 1, 0), ("s", b + 2, 0)]


class Prog(KB):
    def __init__(self, ins, outs, dbg=None):
        super().__init__()
        self.dbg = dbg or {}
        nc = self.nc
        self.i = {}
        for name, shape in ins.items():
            self.i[name] = nc.dram_tensor(name, list(shape), F32, kind="ExternalInput").ap()
        self.o = {}
        for name, shape in outs.items():
            self.o[name] = nc.dram_tensor(name, list(shape), F32, kind="ExternalOutput").ap()
        self.w = self.i
        self._dr = {}
        self.X = [self.dr("Xa", [T, D]), self.dr("Xb", [T, D])]
        self.H = self.dr("H", [T, D])
        self.XT = self.dr("XT", [D, T], BF16)
        self.XTf = self.dr("XTf", [D, T])
        self.MT = self.dr("MT", [D, NU * MEM], BF16)
        self.dbg_out = {}
        for name, shape in self.dbg.items():
            self.dbg_out[name] = nc.dram_tensor("dbg_" + name, list(shape), F32, kind="ExternalOutput").ap()

        self.consts, self.t_consts = self.tile([128, NCONST])
        self.flags, self.t_flags = self.tile([128, 16])
        self.AFF, self.t_aff = self.tile([128, NTILE, NE])
        self.G, self.t_g = self.tile([128, NTILE, NE])
        self.persist()
        self.dma(self.consts, self.i["consts"], writes=[self.t_consts])
        self.dma(self.flags, self.i["flags"], writes=[self.t_flags])
        self.ident = self.consts[:, C_IDENT:C_IDENT + 128]
        self.ones = self.consts[:, C_ONES:C_ONES + 128]

    def dr(self, name, shape, dt=F32):
        if name not in self._dr:
            self._dr[name] = self.nc.dram_tensor(name, list(shape), dt, kind="Internal").ap()
        return self._dr[name]

    def cst(self, name):
        o, n = CONST_LAYOUT[name]
        return self.consts[:, o:o + n]

    def load_w_bf16(self, dst, tok, w_ap):
        wv = w_ap.rearrange("(c p) n -> p c n", p=128)
        for c in range(wv.shape[1]):
            self.dma(dst[:, c, :], wv[:, c, :], writes=[tok], q=POOL)

    def bcast_row(self, dst, tok, row_ap, n):
        self.dma(dst, row_ap.partition_broadcast(128), writes=[tok])

    def stage_ln(self, Xold, Hs, Xnew, lnw=None, lnb=None, do_ln=True, router_w=None, want_f32T=False,
                 xt_out=None, ntok=T):
        self.stage_begin()
        xt_out = self.XT if xt_out is None else xt_out
        ntile = ntok // 128
        xin = self.ring(2, [128, D])
        hin = self.ring(2, [128, D])
        st_r = self.ring(2, [128, 2, 6])
        mv_r = self.ring(2, [128, 4])
        xtb = self.ring(2, [128, 8, 512], BF16)
        if do_ln:
            wrep, t_w = self.tile([128, D])
            brep, t_b = self.tile([128, D])
            self.bcast_row(wrep, t_w, lnw, D)
            self.bcast_row(brep, t_b, lnb, D)
        if router_w is not None:
            wr, t_wr = self.tile([128, NE, D])
            for ex_i in range(NE):
                self.dma(wr[:, ex_i, :], router_w[ex_i].partition_broadcast(128), writes=[t_wr])
            lg_r = self.ring(2, [128, NE])
            rt_r = self.ring(2, [128, D])
            sm_r = self.ring(2, [128, 4])
            ex_r = self.ring(2, [128, NE])
        xT_blk = None
        for i in range(ntile):
            rows = slice(i * 128, (i + 1) * 128)
            x, tx = xin.next()
            self.dma(x, Xold[rows, :], writes=[tx])
            if do_ln:
                h, th = hin.next()
                self.dma(h, Hs[rows, :], writes=[th])
                self.op(DVE, lambda e, x=x, h=h: e.scalar_tensor_tensor(
                    out=x, in0=x, scalar=DN_ALPHA, in1=h, op0=ALU.mult, op1=ALU.add), [tx, th], [tx])
                st, tst = st_r.next()
                for k in range(2):
                    self.op(DVE, lambda e, st=st, x=x, k=k: e.bn_stats(out=st[:, k, :], in_=x[:, k * 512:(k + 1) * 512]),
                            [tx], [tst])
                mv, tmv = mv_r.next()
                self.op(DVE, lambda e, mv=mv, st=st: e.bn_aggr(out=mv[:, 0:2], in_=st.rearrange("p a b -> p (a b)")),
                        [tst], [tmv])
                self.op(DVE, lambda e, mv=mv: e.tensor_scalar(out=mv[:, 2:3], in0=mv[:, 1:2], scalar1=LN_EPS,
                                                             scalar2=None, op0=ALU.add), [tmv], [tmv])
                self.op(ACT, lambda e, mv=mv: e.activation(out=mv[:, 2:3], in_=mv[:, 2:3], func=AF.Sqrt), [tmv], [tmv])
                self.op(DVE, lambda e, mv=mv: e.reciprocal(out=mv[:, 3:4], in_=mv[:, 2:3]), [tmv], [tmv])
                self.op(DVE, lambda e, mv=mv, x=x: e.tensor_scalar(out=x, in0=x, scalar1=mv[:, 0:1], scalar2=mv[:, 3:4],
                                                                  op0=ALU.subtract, op1=ALU.mult), [tx, tmv], [tx])
                self.op(POOL, lambda e, x=x: e.tensor_tensor(out=x, in0=x, in1=wrep, op=ALU.mult), [tx, t_w], [tx])
                self.op(POOL, lambda e, x=x: e.tensor_tensor(out=x, in0=x, in1=brep, op=ALU.add), [tx, t_b], [tx])
                self.dma(Xnew[rows, :], x, reads=[tx])
            if i % 4 == 0:
                xT_blk, t_blk = xtb.next()
            for half in range(2):
                ps, tps = self.ps()
                for k in range(4):
                    c = half * 4 + k
                    self.op(PE, lambda e, ps=ps, x=x, k=k, c=c: e.transpose(
                        out=ps[:, k * 128:(k + 1) * 128], in_=x[:, c * 128:(c + 1) * 128], identity=self.ident),
                        [tx, self.t_consts], [tps])
                dst = xT_blk[:, half * 4:(half + 1) * 4, (i % 4) * 128:(i % 4 + 1) * 128]
                src = ps.rearrange("p (c t) -> p c t", c=4)
                self.copy(ACT, dst, src, [tps], [t_blk])
            if i % 4 == 3 or i == ntile - 1:
                t0 = (i // 4) * 512
                wd = (i % 4 + 1) * 128
                self.dma(xt_out.rearrange("(c p) t -> p c t", p=128)[:, :, t0:t0 + wd], xT_blk[:, :, 0:wd], reads=[t_blk])
            if router_w is not None:
                lg, tlg = lg_r.next()
                for ex_i in range(NE):
                    tmp, ttmp = rt_r.next()
                    self.op(POOL, lambda e, tmp=tmp, x=x, ex_i=ex_i: e.tensor_tensor(
                        out=tmp, in0=x, in1=wr[:, ex_i, :], op=ALU.mult), [tx, t_wr], [ttmp])
                    self.op(DVE, lambda e, tmp=tmp, lg=lg, ex_i=ex_i: e.tensor_reduce(
                        out=lg[:, ex_i:ex_i + 1], in_=tmp, axis=AX.X, op=ALU.add), [ttmp], [tlg])
                ps, tps = lg, tlg
                sm, tsm = sm_r.next()
                ex, tex = ex_r.next()
                self.op(DVE, lambda e, sm=sm, ps=ps: e.tensor_reduce(out=sm[:, 0:1], in_=ps[:, 0:NE], axis=AX.X, op=ALU.max),
                        [tps], [tsm])
                self.op(DVE, lambda e, sm=sm: e.tensor_scalar(out=sm[:, 1:2], in0=sm[:, 0:1], scalar1=-1.0, scalar2=None,
                                                             op0=ALU.mult), [tsm], [tsm])
                self.op(ACT, lambda e, sm=sm, ex=ex, ps=ps: e.activation(out=ex, in_=ps[:, 0:NE], func=AF.Exp,
                                                                         bias=sm[:, 1:2], scale=1.0), [tps, tsm], [tex])
                self.op(DVE, lambda e, sm=sm, ex=ex: e.tensor_reduce(out=sm[:, 2:3], in_=ex, axis=AX.X, op=ALU.add),
                        [tex], [tsm])
                self.op(DVE, lambda e, sm=sm: e.reciprocal(out=sm[:, 3:4], in_=sm[:, 2:3]), [tsm], [tsm])
                self.op(DVE, lambda e, sm=sm, ex=ex, i=i: e.tensor_scalar(out=self.AFF[:, i, :], in0=ex, scalar1=sm[:, 3:4],
                                                                         scalar2=None, op0=ALU.mult),
                        [tex, tsm], [self.t_aff])
        if router_w is not None:
            self.dma(self.o["aff"].rearrange("(i p) e -> p i e", p=128), self.AFF, reads=[self.t_aff])

    def stage_linear(self, W, Y, N, ntok=T, xt=None):
        self.stage_begin()
        xt = self.XT if xt is None else xt
        wt, t_wt = self.tile([128, 8, N], BF16)
        self.load_w_bf16(wt, t_wt, W)
        xr = self.ring(2, [128, 8, 512], BF16)
        yr = self.ring(2, [128, N])
        for tb in range(ntok // 512):
            xb, txb = xr.next()
            self.dma(xb, xt.rearrange("(c p) t -> p c t", p=128)[:, :, tb * 512:(tb + 1) * 512], writes=[txb])
            for tt in range(4):
                y, ty = yr.next()
                for n0 in range(0, N, 512):
                    nw = min(512, N - n0)
                    ps, tps = self.ps()
                    for c in range(8):
                        self.op(PE, lambda e, ps=ps, xb=xb, c=c, tt=tt, n0=n0, nw=nw: e.matmul(
                            ps[:, 0:nw], lhsT=xb[:, c, tt * 128:(tt + 1) * 128], rhs=wt[:, c, n0:n0 + nw],
                            start=(c == 0), stop=(c == 7)), [txb, t_wt], [tps])
                    self.copy(self.ev(), y[:, n0:n0 + nw], ps[:, 0:nw], [tps], [ty])
                r0 = tb * 512 + tt * 128
                self.dma(Y[r0:r0 + 128, :], y, reads=[ty])

    def stage_xattn(self):
        self.stage_begin()
        wq, t_wq = self.tile([128, 8, D], BF16)
        wkv, t_wkv = self.tile([128, 8, 2 * D], BF16)
        wo, t_wo = self.tile([128, 8, D], BF16)
        self.load_w_bf16(wq, t_wq, self.w["ca_w_q"])
        self.load_w_bf16(wkv, t_wkv, self.w["ca_w_kv"])
        self.load_w_bf16(wo, t_wo, self.w["ca_w_out"])
        ones_bf, t_ob = self.tile([128, 128], BF16)
        self.copy(DVE, ones_bf, self.ones, [self.t_consts], [t_ob])
        mt, t_mt = self.tile([128, 8, MEM], BF16)
        KT, t_KT = self.tile([128, 8, MEM], BF16)
        V, t_V = self.tile([128, 2, D], BF16)
        xr = self.ring(2, [128, 8, 512], BF16)
        qT, t_qT = self.tile([128, 8, 512], BF16)
        E_r = self.ring(2, [128, 2, 512], BF16)
        R_r = self.ring(2, [128, 512])
        oT, t_oT = self.tile([128, 8, 512], BF16)
        hr = self.ring(2, [128, D])
        for u in range(NU):
            self.dma(mt, self.MT.rearrange("(c p) t -> p c t", p=128)[:, :, u * MEM:(u + 1) * MEM], writes=[t_mt])
            for oc in range(8):
                ps, tps = self.ps()
                for c in range(8):
                    self.op(PE, lambda e, ps=ps, c=c, oc=oc: e.matmul(ps[:, 0:MEM], lhsT=wkv[:, c, oc * 128:(oc + 1) * 128],
                                                                      rhs=mt[:, c, :], start=(c == 0), stop=(c == 7)),
                            [t_wkv, t_mt], [tps])
                self.copy(self.ev(), KT[:, oc, :], ps[:, 0:MEM], [tps], [t_KT])
            for mc in range(2):
                for dh in range(2):
                    ps, tps = self.ps()
                    for c in range(8):
                        self.op(PE, lambda e, ps=ps, c=c, mc=mc, dh=dh: e.matmul(
                            ps, lhsT=mt[:, c, mc * 128:(mc + 1) * 128], rhs=wkv[:, c, D + dh * 512:D + (dh + 1) * 512],
                            start=(c == 0), stop=(c == 7)), [t_wkv, t_mt], [tps])
                    self.copy(self.ev(), V[:, mc, dh * 512:(dh + 1) * 512], ps, [tps], [t_V])
            for tbu in range(TU // 512):
                tb = u * (TU // 512) + tbu
                xb, txb = xr.next()
                self.dma(xb, self.XT.rearrange("(c p) t -> p c t", p=128)[:, :, tb * 512:(tb + 1) * 512], writes=[txb])
                for oc in range(8):
                    ps, tps = self.ps()
                    for c in range(8):
                        self.op(PE, lambda e, ps=ps, c=c, oc=oc, xb=xb: e.matmul(
                            ps, lhsT=wq[:, c, oc * 128:(oc + 1) * 128], rhs=xb[:, c, :], start=(c == 0), stop=(c == 7)),
                            [t_wq, txb], [tps])
                    self.copy(self.ev(), qT[:, oc, :], ps, [tps], [t_qT], scale=1.0 / 16.0)
                for hh in range(4):
                    E, tE = E_r.next()
                    for mc in range(2):
                        ps, tps = self.ps()
                        for j in range(2):
                            self.op(PE, lambda e, ps=ps, mc=mc, j=j, hh=hh: e.matmul(
                                ps, lhsT=KT[:, hh * 2 + j, mc * 128:(mc + 1) * 128], rhs=qT[:, hh * 2 + j, :],
                                start=(j == 0), stop=(j == 1)), [t_KT, t_qT], [tps])
                        self.op(ACT, lambda e, ps=ps, E=E, mc=mc: e.activation(out=E[:, mc, :], in_=ps, func=AF.Exp),
                                [tps], [tE])
                    psd, tpsd = self.ps()
                    for mc in range(2):
                        self.op(PE, lambda e, psd=psd, E=E, mc=mc: e.matmul(psd, lhsT=ones_bf, rhs=E[:, mc, :],
                                                                            start=(mc == 0), stop=(mc == 1)),
                                [t_ob, tE], [tpsd])
                    R, tR = R_r.next()
                    self.op(DVE, lambda e, R=R, psd=psd: e.reciprocal(out=R, in_=psd), [tpsd], [tR])
                    for j in range(2):
                        ps, tps = self.ps()
                        for mc in range(2):
                            self.op(PE, lambda e, ps=ps, E=E, mc=mc, j=j, hh=hh: e.matmul(
                                ps, lhsT=V[:, mc, hh * 256 + j * 128:hh * 256 + (j + 1) * 128], rhs=E[:, mc, :],
                                start=(mc == 0), stop=(mc == 1)), [t_V, tE], [tps])
                        self.op(DVE, lambda e, ps=ps, R=R, j=j, hh=hh: e.tensor_tensor(
                            out=oT[:, hh * 2 + j, :], in0=ps, in1=R, op=ALU.mult), [tps, tR], [t_oT])
                for tt in range(4):
                    h, th = hr.next()
                    for dh in range(2):
                        ps, tps = self.ps()
                        for c in range(8):
                            self.op(PE, lambda e, ps=ps, c=c, tt=tt, dh=dh: e.matmul(
                                ps, lhsT=oT[:, c, tt * 128:(tt + 1) * 128], rhs=wo[:, c, dh * 512:(dh + 1) * 512],
                                start=(c == 0), stop=(c == 7)), [t_oT, t_wo], [tps])
                        self.copy(self.ev(), h[:, dh * 512:(dh + 1) * 512], ps, [tps], [th])
                    r0 = tb * 512 + tt * 128
                    self.dma(self.H[r0:r0 + 128, :], h, reads=[th])

    def stage_topk(self):
        self.stage_begin()
        aff_all = self.i["aff_all"]
        self.dma(self.AFF, self.i["aff_loc"].rearrange("(i p) e -> p i e", p=128), writes=[self.t_aff])
        JP, JS = 128, 256
        A, tA = self.tile([128, JP + JS, NE])
        for c in range(4):
            self.dma(A[:, c * 32:(c + 1) * 32, :],
                     aff_all[c * T:c * T + 4096, :].rearrange("(p j) e -> p j e", j=32), writes=[tA])
            self.dma(A[:, JP + c * 16:JP + (c + 1) * 16, :],
                     aff_all[c * T + 4096:(c + 1) * T, :].rearrange("(p j) e -> p j e", j=16), writes=[tA])
        for c in range(4, 8):
            o = JP + 64 + (c - 4) * 48
            self.dma(A[:, o:o + 48, :], aff_all[c * T:(c + 1) * T, :].rearrange("(p j) e -> p j e", j=48),
                     writes=[tA])
        cmp_, tcmp = self.tile([128, JP + JS, NE])
        lo, tlo = self.tile([128, 2, NE])
        hi, thi = self.tile([128, 2, NE])
        mid, tmid = self.tile([128, 2, NE])
        cnt, tcnt = self.tile([128, 2, NE])
        capt, tcap = self.tile([128, 2, NE])
        m, tm = self.tile([128, 2, NE])
        t1, tt1 = self.tile([128, 2, NE])
        self.op(DVE, lambda e: e.memset(lo, 0.0), [], [tlo])
        self.op(DVE, lambda e: e.memset(hi, 2.0), [], [thi])
        self.op(DVE, lambda e: e.memset(mid, 1.0), [], [tmid])
        self.op(DVE, lambda e: e.memset(capt[:, 0, :], float(CAP_P)), [], [tcap])
        self.op(DVE, lambda e: e.memset(capt[:, 1, :], float(CAP_S)), [], [tcap])
        segs = [(0, 0, JP), (1, JP, JS)]
        for it in range(40):
            for g, j0, jn in segs:
                self.op(DVE, lambda e, g=g, j0=j0, jn=jn: e.tensor_tensor(
                    out=cmp_[:, j0:j0 + jn, :], in0=A[:, j0:j0 + jn, :],
                    in1=mid[:, g:g + 1, :].to_broadcast([128, jn, NE]), op=ALU.is_ge), [tA, tmid], [tcmp])
                self.op(DVE, lambda e, g=g, j0=j0, jn=jn: e.tensor_reduce(
                    out=cnt[:, g, :], in_=cmp_[:, j0:j0 + jn, :].rearrange("p j e -> p e j"), axis=AX.X, op=ALU.add),
                    [tcmp], [tcnt])
            ps, tps = self.ps()
            self.op(PE, lambda e, ps=ps: e.matmul(ps[:, 0:2 * NE], lhsT=self.ones, rhs=cnt.rearrange("p g e -> p (g e)"),
                                                  start=True, stop=True), [tcnt, self.t_consts], [tps])
            self.op(DVE, lambda e, ps=ps: e.tensor_tensor(out=m.rearrange("p g e -> p (g e)"), in0=ps[:, 0:2 * NE],
                                                          in1=capt.rearrange("p g e -> p (g e)"), op=ALU.is_ge),
                    [tps, tcap], [tm])
            self.op(DVE, lambda e: e.tensor_tensor(out=t1, in0=m, in1=mid, op=ALU.mult), [tm, tmid], [tt1])
            self.op(DVE, lambda e: e.tensor_tensor(out=lo, in0=lo, in1=t1, op=ALU.max), [tlo, tt1], [tlo])
            self.op(DVE, lambda e: e.scalar_tensor_tensor(out=t1, in0=m, scalar=4.0, in1=mid, op0=ALU.mult, op1=ALU.add),
                    [tm, tmid], [tt1])
            self.op(DVE, lambda e: e.tensor_tensor(out=hi, in0=hi, in1=t1, op=ALU.min), [thi, tt1], [thi])
            self.op(DVE, lambda e: e.tensor_tensor(out=mid, in0=lo, in1=hi, op=ALU.add), [tlo, thi], [tmid])
            self.op(DVE, lambda e: e.tensor_scalar(out=mid, in0=mid, scalar1=0.5, scalar2=None, op0=ALU.mult),
                    [tmid], [tmid])
        gp, tgp = self.tile([128, 16, NE])
        gs, tgs = self.tile([128, 16, NE])
        for u in range(NU):
            Au = self.AFF[:, u * 16:(u + 1) * 16, :]
            for g, gt, tgt in ((0, gp, tgp), (1, gs, tgs)):
                self.op(DVE, lambda e, g=g, gt=gt, Au=Au: e.tensor_tensor(
                    out=gt, in0=Au, in1=lo[:, g:g + 1, :].to_broadcast([128, 16, NE]), op=ALU.is_ge),
                    [self.t_aff, tlo], [tgt])
                self.op(DVE, lambda e, gt=gt, Au=Au: e.tensor_tensor(out=gt, in0=gt, in1=Au, op=ALU.mult),
                        [self.t_aff, tgt], [tgt])
            fp = self.flags[:, 6 + u:7 + u]
            nfp = self.flags[:, 9 + u:10 + u]
            self.op(DVE, lambda e, fp=fp: e.tensor_scalar(out=gp, in0=gp, scalar1=fp, scalar2=None, op0=ALU.mult),
                    [tgp, self.t_flags], [tgp])
            self.op(DVE, lambda e, nfp=nfp, u=u: e.scalar_tensor_tensor(
                out=self.G[:, u * 16:(u + 1) * 16, :], in0=gs, scalar=nfp, in1=gp, op0=ALU.mult, op1=ALU.add),
                [tgs, tgp, self.t_flags], [self.t_g])
        if "thr" in self.dbg_out:
            self.dma(self.dbg_out["thr"], lo.rearrange("p g e -> p (g e)"), reads=[tlo])

    def stage_moe(self):
        self.stage_begin()
        w_in = self.w["moe_w_in"]
        w_out = self.w["moe_w_out"]
        xT, t_xT = self.tile([128, 8, TU], BF16)
        acc, t_acc = self.tile([128, 16, D])
        wi_r = self.ring(2, [128, 8, 1024], BF16)
        wo_r = self.ring(2, [128, 4, D], BF16)
        s_r = self.ring(2, [128, 512])
        h_r = self.ring(2, [128, 4, 512], BF16)
        for u in range(NU):
            self.dma(xT, self.XT.rearrange("(c p) t -> p c t", p=128)[:, :, u * TU:(u + 1) * TU], writes=[t_xT])
            for ex in range(NE):
                for qd in range(4):
                    f0 = qd * 512
                    wi, twi = wi_r.next()
                    wo, two = wo_r.next()
                    self.dma(wi[:, :, 0:512], w_in[ex][:, f0:f0 + 512].rearrange("(c p) n -> p c n", p=128),
                             writes=[twi], q=POOL)
                    self.dma(wi[:, :, 512:1024],
                             w_in[ex][:, DEXP + f0:DEXP + f0 + 512].rearrange("(c p) n -> p c n", p=128),
                             writes=[twi], q=POOL)
                    self.dma(wo, w_out[ex][f0:f0 + 512, :].rearrange("(c p) n -> p c n", p=128), writes=[two], q=POOL)
                    first = (ex == 0 and qd == 0)
                    for tb in range(TU // 512):
                        hT, thT = h_r.next()
                        for fc in range(4):
                            psg, tpsg = self.ps()
                            psu, tpsu = self.ps()
                            for c in range(8):
                                self.op(PE, lambda e, psg=psg, wi=wi, c=c, fc=fc, tb=tb: e.matmul(
                                    psg, lhsT=wi[:, c, fc * 128:(fc + 1) * 128], rhs=xT[:, c, tb * 512:(tb + 1) * 512],
                                    start=(c == 0), stop=(c == 7)), [twi, t_xT], [tpsg])
                            for c in range(8):
                                self.op(PE, lambda e, psu=psu, wi=wi, c=c, fc=fc, tb=tb: e.matmul(
                                    psu, lhsT=wi[:, c, 512 + fc * 128:512 + (fc + 1) * 128],
                                    rhs=xT[:, c, tb * 512:(tb + 1) * 512], start=(c == 0), stop=(c == 7)),
                                    [twi, t_xT], [tpsu])
                            s, ts = s_r.next()
                            self.op(ACT, lambda e, s=s, psg=psg: e.activation(out=s, in_=psg, func=AF.Silu), [tpsg], [ts])
                            self.op(DVE, lambda e, s=s, psu=psu, hT=hT, fc=fc: e.tensor_tensor(
                                out=hT[:, fc, :], in0=psu, in1=s, op=ALU.mult), [tpsu, ts], [thT])
                        for tt in range(4):
                            ti = tb * 4 + tt
                            gcol = self.G[:, u * 16 + ti, ex:ex + 1]
                            for dh in range(2):
                                ps, tps = self.ps()
                                for fc in range(4):
                                    self.op(PE, lambda e, ps=ps, hT=hT, wo=wo, fc=fc, tt=tt, dh=dh: e.matmul(
                                        ps, lhsT=hT[:, fc, tt * 128:(tt + 1) * 128], rhs=wo[:, fc, dh * 512:(dh + 1) * 512],
                                        start=(fc == 0), stop=(fc == 3)), [thT, two], [tps])
                                dst = acc[:, ti, dh * 512:(dh + 1) * 512]
                                if first:
                                    self.op(DVE, lambda e, ps=ps, dst=dst, gcol=gcol: e.tensor_scalar(
                                        out=dst, in0=ps, scalar1=gcol, scalar2=None, op0=ALU.mult),
                                        [tps, self.t_g], [t_acc])
                                else:
                                    self.op(DVE, lambda e, ps=ps, dst=dst, gcol=gcol: e.scalar_tensor_tensor(
                                        out=dst, in0=ps, scalar=gcol, in1=dst, op0=ALU.mult, op1=ALU.add),
                                        [tps, self.t_g, t_acc], [t_acc])
            for ti in range(16):
                r0 = u * TU + ti * 128
                self.dma(self.H[r0:r0 + 128, :], acc[:, ti, :], reads=[t_acc])

    def dump(self, name, src):
        if name in self.dbg_out:
            self.barrier()
            self.dma(self.dbg_out[name], src)


def _tt(self, eng, out, in0, in1, op, R, W):
    self.op(eng, lambda e: e.tensor_tensor(out=out, in0=in0, in1=in1, op=op), R, W)


def _ts(self, eng, out, in0, s1, op0, R, W, s2=None, op1=None):
    if op1 is None:
        self.op(eng, lambda e: e.tensor_scalar(out=out, in0=in0, scalar1=s1, scalar2=None, op0=op0), R, W)
    else:
        self.op(eng, lambda e: e.tensor_scalar(out=out, in0=in0, scalar1=s1, scalar2=s2, op0=op0, op1=op1), R, W)


def _stt(self, out, in0, scalar, in1, op0, op1, R, W):
    self.op(DVE, lambda e: e.scalar_tensor_tensor(out=out, in0=in0, scalar=scalar, in1=in1, op0=op0, op1=op1), R, W)


def _act(self, out, in_, func, R, W, scale=1.0, bias=None):
    if bias is None:
        self.op(ACT, lambda e: e.activation(out=out, in_=in_, func=func, scale=scale), R, W)
    else:
        self.op(ACT, lambda e: e.activation(out=out, in_=in_, func=func, scale=scale, bias=bias), R, W)


def _mm(self, ps, lhsT, rhs, start, stop, R, W):
    self.op(PE, lambda e: e.matmul(ps, lhsT=lhsT, rhs=rhs, start=start, stop=stop), R, W)


def _tr(self, ps, in_, R, W):
    ident = self.ident
    n = in_.shape[0]
    self.op(PE, lambda e: e.transpose(out=ps, in_=in_, identity=ident[:n, :n]), list(R) + [self.t_consts], W)


def _red(self, eng, out, in_, op, R, W):
    self.op(eng, lambda e: e.tensor_reduce(out=out, in_=in_, axis=AX.X, op=op), R, W)


def _recip(self, out, in_, R, W):
    self.op(DVE, lambda e: e.reciprocal(out=out, in_=in_), R, W)


def _memset(self, eng, ap, val, W):
    self.op(eng, lambda e: e.memset(ap, val), [], W)


for _f in (_tt, _ts, _stt, _act, _mm, _tr, _red, _recip, _memset):
    setattr(Prog, _f.__name__[1:], _f)


def _b3(ap, n):
    m = ap.shape[1]
    return ap.rearrange("p (o m) -> p o m", o=1).to_broadcast([128, n, m])


def _v3(ap, n):
    return ap.rearrange("p (h m) -> p h m", h=n)


def stage_hgrn(self, j, d, PROJ, OF, MIX):
    self.stage_begin()
    C = self.cst
    tri = C("tri_f") if d == 0 else C("tri_b")
    blk = C("blk")
    ind = C("ind")
    tc_ = self.t_consts
    lbl = self.w["ev_lb_logits"]
    if j == 1:
        l0, tl0 = self.tile([128, 512])
        lbr, tlb = self.tile([128, 512])
        oml, tom = self.tile([128, 512])
        self.bcast_row(l0, tl0, lbl[d, 0], 512)
        self.bcast_row(lbr, tlb, lbl[d, 1], 512)
        self.tt(DVE, lbr, lbr, l0, ALU.subtract, [tlb, tl0], [tlb])
        self.act(lbr, lbr, AF.Sigmoid, [tlb], [tlb])
        self.ts(DVE, oml, lbr, -1.0, ALU.mult, [tlb], [tom], s2=1.0, op1=ALU.add)
    if d == 1:
        nrm, tnrm = self.tile([128, 512])
        self.bcast_row(nrm, tnrm, self.w["ev_norm_a"], 512)
    aq_r = self.ring(2, [128, 512])
    ai_r = self.ring(2, [128, 512])
    z_r = self.ring(2, [128, 512])
    sg_r = self.ring(2, [128, 512])
    lf_r = self.ring(2, [128, 512])
    kin_r = self.ring(2, [128, 512])
    qin_r = self.ring(2, [128, 512])
    e_r = self.ring(3, [128, 512])
    qt_r = self.ring(2, [128, 512])
    kt_r = self.ring(2, [128, 512])
    kh_r = self.ring(2, [128, 512], BF16)
    vb_r = self.ring(2, [128, 512], BF16)
    pl_r = self.ring(2, [128, 8])
    qT_r = self.ring(2, [128, 4, 128], BF16)
    kT_r = self.ring(2, [128, 4, 128], BF16)
    sc_r = self.ring(3, [128, 128], BF16)
    o_r = self.ring(2, [128, 512])
    if d == 1:
        ag_r = self.ring(2, [128, 512])
        of_r = self.ring(2, [128, 512])
        sq_r = self.ring(2, [128, 512])
        st_r = self.ring(2, [128, 8])
    S, tS = self.tile([128, 4, 128])
    Sb, tSb = self.tile([128, 4, 128], BF16)
    self.memset(DVE, S, 0.0, [tS])
    self.memset(DVE, Sb, 0.0, [tSb])
    order = list(range(NTILE)) if d == 0 else list(reversed(range(NTILE)))
    for i in order:
        u = i // 16
        rows = slice(i * 128, (i + 1) * 128)
        if (d == 0 and i % 16 == 0) or (d == 1 and i % 16 == 15):
            fcol = self.flags[:, (u if d == 0 else 3 + u):(u if d == 0 else 3 + u) + 1]
            self.ts(DVE, S, S, fcol, ALU.mult, [tS, self.t_flags], [tS])
            self.copy(ACT, Sb, S, [tS], [tSb])
        aq, taq = aq_r.next()
        ai, tai = ai_r.next()
        z, tz = z_r.next()
        self.dma(aq, PROJ[rows, 0:512], writes=[taq])
        self.dma(ai, PROJ[rows, 512:1024], writes=[tai])
        self.dma(z, PROJ[rows, 1024 + d * 512:1536 + d * 512], writes=[tz])
        sg, tsg = sg_r.next()
        lf, tlf = lf_r.next()
        kin, tkin = kin_r.next()
        qin, tqin = qin_r.next()
        self.act(sg, z, AF.Sigmoid, [tz], [tsg])
        self.ts(DVE, kin, sg, -1.0, ALU.mult, [tsg], [tkin], s2=1.0, op1=ALU.add)
        if j == 1:
            self.tt(DVE, kin, kin, oml, ALU.mult, [tkin, tom], [tkin])
            self.tt(DVE, sg, sg, oml, ALU.mult, [tsg, tom], [tsg])
            self.tt(DVE, sg, sg, lbr, ALU.add, [tsg, tlb], [tsg])
        self.act(lf, sg, AF.Ln, [tsg], [tlf])
        self.act(qin, aq, AF.Silu, [taq], [tqin])
        pc, tpc = self.ps()
        self.mm(pc, tri, lf, True, True, [tc_, tlf], [tpc])
        pt, tpt = self.ps()
        self.mm(pt, blk, lf, True, True, [tc_, tlf], [tpt])
        pp, tpp = self.ps()
        for h in range(4):
            self.mm(pp[:, 2 * h:2 * h + 2], lf[:, h * 128:(h + 1) * 128], ind[:, 0:2], True, True, [tlf, tc_], [tpp])
        pl, tpl = pl_r.next()
        self.act(pl, pp[:, 0:8], AF.Exp, [tpp], [tpl])
        e1, te1 = e_r.next()
        self.act(e1, pc, AF.Exp, [tpc], [te1])
        qt, tqt = qt_r.next()
        self.tt(DVE, qt, qin, e1, ALU.mult, [tqin, te1], [tqt])
        e2, te2 = e_r.next()
        self.act(e2, pc, AF.Exp, [tpc], [te2], scale=-1.0)
        kt, tkt = kt_r.next()
        self.tt(DVE, kt, kin, e2, ALU.mult, [tkin, te2], [tkt])
        e3, te3 = e_r.next()
        self.act(e3, pt, AF.Exp, [tpt], [te3])
        kh, tkh = kh_r.next()
        self.tt(DVE, kh, kt, e3, ALU.mult, [tkt, te3], [tkh])
        vb, tvb = vb_r.next()
        self.copy(ACT, vb, ai, [tai], [tvb])
        qT, tqT = qT_r.next()
        kT, tkT = kT_r.next()
        for (src, tsrc, dst, tdst) in ((qt, tqt, qT, tqT), (kt, tkt, kT, tkT)):
            ps, tps = self.ps()
            for h in range(4):
                self.tr(ps[:, h * 128:(h + 1) * 128], src[:, h * 128:(h + 1) * 128], [tsrc], [tps])
            self.copy(ACT, dst, ps.rearrange("p (h t) -> p h t", h=4), [tps], [tdst])
        o, to = o_r.next()
        for h in range(4):
            hs = slice(h * 128, (h + 1) * 128)
            psc, tpsc = self.ps()
            self.mm(psc[:, 0:128], kT[:, h, :], qT[:, h, :], True, True, [tkT, tqT], [tpsc])
            scm, tscm = sc_r.next()
            self.tt(DVE, scm, psc[:, 0:128], tri, ALU.mult, [tpsc, tc_], [tscm])
            py, tpy = self.ps()
            self.mm(py[:, 0:128], scm, vb[:, hs], True, False, [tscm, tvb], [tpy])
            cs = (0, 1) if d == 0 else (1, 0)
            for ci, c in enumerate(cs):
                cr = slice(c * 64, (c + 1) * 64)
                self.mm(py[cr, 0:128], qT[:, h, cr], Sb[:, h, :], False, ci == 1, [tqT, tSb], [tpy])
                pn, tpn = self.ps()
                self.mm(pn[:, 0:128], kh[cr, hs], vb[cr, hs], True, True, [tkh, tvb], [tpn])
                self.stt(S[:, h, :], S[:, h, :], pl[:, 2 * h + c:2 * h + c + 1], pn[:, 0:128], ALU.mult, ALU.add,
                         [tS, tpl, tpn], [tS])
                self.copy(ACT, Sb[:, h, :], S[:, h, :], [tS], [tSb])
            self.copy(ACT, o[:, hs], py[:, 0:128], [tpy], [to])
        if d == 0:
            self.dma(OF[rows, 0:512], o, reads=[to])
        else:
            of, tof = of_r.next()
            ag, tag = ag_r.next()
            self.dma(of, OF[rows, 0:512], writes=[tof])
            self.dma(ag, PROJ[rows, 2048:2560], writes=[tag])
            self.tt(DVE, o, o, of, ALU.add, [to, tof], [to])
            sq, tsq = sq_r.next()
            st, tst = st_r.next()
            self.tt(POOL, sq, o, o, ALU.mult, [to], [tsq])
            self.red(DVE, st[:, 0:4], _v3(sq, 4), ALU.add, [tsq], [tst])
            self.ts(DVE, st[:, 0:4], st[:, 0:4], 1.0 / 128.0, ALU.mult, [tst], [tst], s2=1e-6, op1=ALU.add)
            self.act(st[:, 0:4], st[:, 0:4], AF.Sqrt, [tst], [tst])
            self.recip(st[:, 4:8], st[:, 0:4], [tst], [tst])
            self.tt(DVE, _v3(o, 4), _v3(o, 4), st[:, 4:8].rearrange("p (h o) -> p h o", o=1).to_broadcast([128, 4, 128]),
                    ALU.mult, [to, tst], [to])
            self.tt(POOL, o, o, nrm, ALU.mult, [to, tnrm], [to])
            self.act(ag, ag, AF.Silu, [tag], [tag])
            self.tt(DVE, o, o, ag, ALU.mult, [to, tag], [to])
            self.dma(MIX[rows, 0:512], o, reads=[to])


Prog.stage_hgrn = stage_hgrn


def stage_ret(self, d, PROJ, OF, MIX):
    self.stage_begin()
    C = self.cst
    tc_ = self.t_consts
    rope = self.i["rope"]
    o0 = CONST_LAYOUT["rmask%d0" % d][0]
    maskT = self.consts[:, o0:o0 + 512].rearrange("p (h t) -> p h t", h=4)
    q0 = CONST_LAYOUT["rqd%d0" % d][0]
    qdr = self.consts[:, q0:q0 + 512].rearrange("p (h t) -> p h t", h=4)
    k0 = CONST_LAYOUT["rkd"][0] + d * 4
    kdc = self.consts[:, k0:k0 + 4]
    if d == 1:
        nrm, tnrm = self.tile([128, 512])
        self.bcast_row(nrm, tnrm, self.w["ev_norm_b"], 512)
    q_r = self.ring(2, [128, 512])
    k_r = self.ring(2, [128, 512])
    v_r = self.ring(2, [128, 512])
    rp_r = self.ring(2, [128, 192])
    qr_r = self.ring(2, [128, 512])
    kr_r = self.ring(2, [128, 512])
    tmp_r = self.ring(2, [128, 512])
    kd_r = self.ring(2, [128, 512], BF16)
    vb_r = self.ring(2, [128, 512], BF16)
    qT_r = self.ring(2, [128, 4, 128], BF16)
    qdT_r = self.ring(2, [128, 4, 128], BF16)
    kT_r = self.ring(2, [128, 4, 128], BF16)
    sc_r = self.ring(3, [128, 128], BF16)
    o_r = self.ring(2, [128, 512])
    if d == 1:
        bg_r = self.ring(2, [128, 512])
        of_r = self.ring(2, [128, 512])
        sq_r = self.ring(2, [128, 512])
        st_r = self.ring(2, [128, 12])
    R, tR = self.tile([128, 4, 128])
    Rb, tRb = self.tile([128, 4, 128], BF16)
    self.memset(DVE, R, 0.0, [tR])
    self.memset(DVE, Rb, 0.0, [tRb])
    KSC = 128.0 ** -0.5
    order = list(range(NTILE)) if d == 0 else list(reversed(range(NTILE)))
    for i in order:
        u = i // 16
        rows = slice(i * 128, (i + 1) * 128)
        if (d == 0 and i % 16 == 0) or (d == 1 and i % 16 == 15):
            fc = u if d == 0 else 3 + u
            self.ts(DVE, R, R, self.flags[:, fc:fc + 1], ALU.mult, [tR, self.t_flags], [tR])
            self.copy(ACT, Rb, R, [tR], [tRb])
        q, tq = q_r.next()
        k, tk = k_r.next()
        v, tv = v_r.next()
        rp, trp = rp_r.next()
        self.dma(q, PROJ[rows, 2560:3072], writes=[tq])
        self.dma(k, PROJ[rows, 3072:3584], writes=[tk])
        self.dma(v, PROJ[rows, 3584:4096], writes=[tv])
        self.dma(rp, rope[rows, :], writes=[trp])
        cosb = _b3(rp[:, 0:128], 4)
        sinb = rp[:, 128:192].rearrange("p (o m) -> p o m", o=1).to_broadcast([128, 4, 64])
        outs = []
        for (src, tsrc, ring) in ((q, tq, qr_r), (k, tk, kr_r)):
            dst, tdst = ring.next()
            tmp, ttmp = tmp_r.next()
            sv = src.rearrange("p (h i two) -> p h i two", h=4, two=2)
            dv = dst.rearrange("p (h i two) -> p h i two", h=4, two=2)
            self.stt(dv[:, :, :, 0], sv[:, :, :, 1], -1.0, sinb, ALU.mult, ALU.mult, [tsrc, trp], [tdst])
            self.tt(DVE, dv[:, :, :, 1], sv[:, :, :, 0], sinb, ALU.mult, [tsrc, trp], [tdst])
            self.tt(POOL, _v3(tmp, 4), _v3(src, 4), cosb, ALU.mult, [tsrc, trp], [ttmp])
            self.tt(DVE, dst, dst, tmp, ALU.add, [tdst, ttmp], [tdst])
            outs.append((dst, tdst))
        (qr, tqr), (kr, tkr) = outs
        kd, tkd = kd_r.next()
        self.stt(_v3(kd, 4), _v3(kr, 4), KSC, kdc.rearrange("p (h o) -> p h o", o=1).to_broadcast([128, 4, 128]),
                 ALU.mult, ALU.mult, [tkr, tc_], [tkd])
        vb, tvb = vb_r.next()
        self.copy(ACT, vb, v, [tv], [tvb])
        qT, tqT = qT_r.next()
        kT, tkT = kT_r.next()
        qdT, tqdT = qdT_r.next()
        ps, tps = self.ps()
        for h in range(4):
            self.tr(ps[:, h * 128:(h + 1) * 128], qr[:, h * 128:(h + 1) * 128], [tqr], [tps])
        self.copy(ACT, qT, ps.rearrange("p (h t) -> p h t", h=4), [tps], [tqT])
        self.tt(DVE, qdT, ps.rearrange("p (h t) -> p h t", h=4), qdr, ALU.mult, [tps, tc_], [tqdT])
        ps, tps = self.ps()
        for h in range(4):
            self.tr(ps[:, h * 128:(h + 1) * 128], kr[:, h * 128:(h + 1) * 128], [tkr], [tps])
        self.copy(ACT, kT, ps.rearrange("p (h t) -> p h t", h=4), [tps], [tkT], scale=KSC)
        o, to = o_r.next()
        for h in range(4):
            hs = slice(h * 128, (h + 1) * 128)
            psc, tpsc = self.ps()
            self.mm(psc[:, 0:128], kT[:, h, :], qT[:, h, :], True, True, [tkT, tqT], [tpsc])
            scm, tscm = sc_r.next()
            self.tt(DVE, scm, psc[:, 0:128], maskT[:, h, :], ALU.mult, [tpsc, tc_], [tscm])
            py, tpy = self.ps()
            self.mm(py[:, 0:128], scm, vb[:, hs], True, False, [tscm, tvb], [tpy])
            self.mm(py[:, 0:128], qdT[:, h, :], Rb[:, h, :], False, True, [tqdT, tRb], [tpy])
            pn, tpn = self.ps()
            self.mm(pn[:, 0:128], kd[:, hs], vb[:, hs], True, True, [tkd, tvb], [tpn])
            self.stt(R[:, h, :], R[:, h, :], float(_GAM[h] ** 128.0), pn[:, 0:128], ALU.mult, ALU.add, [tR, tpn], [tR])
            self.copy(ACT, Rb[:, h, :], R[:, h, :], [tR], [tRb])
            self.copy(ACT, o[:, hs], py[:, 0:128], [tpy], [to])
        if d == 0:
            self.dma(OF[rows, 512:1024], o, reads=[to])
        else:
            of, tof = of_r.next()
            bg, tbg = bg_r.next()
            self.dma(of, OF[rows, 512:1024], writes=[tof])
            self.dma(bg, PROJ[rows, 4096:4608], writes=[tbg])
            self.tt(DVE, o, o, of, ALU.add, [to, tof], [to])
            sq, tsq = sq_r.next()
            st, tst = st_r.next()
            self.red(DVE, st[:, 8:12], _v3(o, 4), ALU.add, [to], [tst])
            self.ts(DVE, st[:, 8:12], st[:, 8:12], 1.0 / 128.0, ALU.mult, [tst], [tst])
            self.tt(DVE, _v3(o, 4), _v3(o, 4), st[:, 8:12].rearrange("p (h o) -> p h o", o=1).to_broadcast([128, 4, 128]),
                    ALU.subtract, [to, tst], [to])
            self.tt(POOL, sq, o, o, ALU.mult, [to], [tsq])
            self.red(DVE, st[:, 0:4], _v3(sq, 4), ALU.add, [tsq], [tst])
            self.ts(DVE, st[:, 0:4], st[:, 0:4], 1.0 / 128.0, ALU.mult, [tst], [tst], s2=1e-6, op1=ALU.add)
            self.act(st[:, 0:4], st[:, 0:4], AF.Sqrt, [tst], [tst])
            self.recip(st[:, 4:8], st[:, 0:4], [tst], [tst])
            self.tt(DVE, _v3(o, 4), _v3(o, 4), st[:, 4:8].rearrange("p (h o) -> p h o", o=1).to_broadcast([128, 4, 128]),
                    ALU.mult, [to, tst], [to])
            self.tt(POOL, o, o, nrm, ALU.mult, [to, tnrm], [to])
            self.act(bg, bg, AF.Silu, [tbg], [tbg])
            self.tt(DVE, o, o, bg, ALU.mult, [to, tbg], [to])
            self.dma(MIX[rows, 512:1024], o, reads=[to])


Prog.stage_ret = stage_ret


def even_mixer(self, j):
    PROJ = self.dr("PROJ", [T, 4608])
    OF = self.dr("OF", [T, D])
    MIX = self.dr("MIX", [T, D])
    self.stage_linear(self.w["ev_w_in"], PROJ, 4608)
    self.stage_hgrn(j, 0, PROJ, OF, MIX)
    self.stage_ret(0, PROJ, OF, MIX)
    self.stage_hgrn(j, 1, PROJ, OF, MIX)
    self.stage_ret(1, PROJ, OF, MIX)
    XT2 = self.dr("XT2", [D, T], BF16)
    self.stage_ln(MIX, None, None, do_ln=False, xt_out=XT2)
    self.stage_linear(self.w["ev_w_out"], self.H, D, xt=XT2)


Prog.even_mixer = even_mixer


def stage_rwkv_proj(self, RR, KK, VV, LWP, APR, GG):
    self.stage_begin()
    w = self.w
    XTfv = self.XT.rearrange("(c p) t -> p c t", p=128)
    mu, tmu = self.tile([128, 48])
    self.dma(mu, w["od_mu_t"], writes=[tmu])
    wr = []
    for p in range(3):
        wt, twt = self.tile([128, 8, D], BF16)
        self.load_w_bf16(wt, twt, w["od_w_rkv"][p])
        wr.append((wt, twt))
    l1 = []
    for nm in ("od_w1c", "od_a1c", "od_g1"):
        wt, twt = self.tile([128, 8, 128], BF16)
        self.load_w_bf16(wt, twt, w[nm])
        l1.append((wt, twt))
    l2 = []
    for nm in ("od_w2c", "od_a2c", "od_g2"):
        wt, twt = self.tile([128, D], BF16)
        self.dma(wt, w[nm], writes=[twt], q=POOL)
        l2.append((wt, twt))
    xw, txw = self.tile([128, 8, 514], BF16)
    tmp, ttmp = self.tile([128, 8, 512])
    xx, txx = self.tile([128, 8, 512])
    xm = [self.tile([128, 8, 512], BF16) for _ in range(6)]
    hT = [self.tile([128, 512], BF16) for _ in range(3)]
    y_r = self.ring(2, [128, D])
    nblk = T // 512
    for tb in range(nblk):
        t0 = tb * 512
        lo = max(t0 - 1, 0)
        hi = min(t0 + 513, T)
        if tb == 0:
            self.memset(DVE, xw[:, :, 0:1], 0.0, [txw])
        if tb == nblk - 1:
            self.memset(DVE, xw[:, :, 513:514], 0.0, [txw])
        self.dma(xw[:, :, lo - (t0 - 1):hi - (t0 - 1)], XTfv[:, :, lo:hi], writes=[txw])
        if t0 % TU == 0 and tb > 0:
            u = t0 // TU
            self.ts(DVE, xw[:, :, 0:1], xw[:, :, 0:1], self.flags[:, u:u + 1], ALU.mult, [txw, self.t_flags], [txw])
        if (t0 + 512) % TU == 0 and tb < nblk - 1:
            u = t0 // TU
            self.ts(DVE, xw[:, :, 513:514], xw[:, :, 513:514], self.flags[:, 3 + u:4 + u], ALU.mult,
                    [txw, self.t_flags], [txw])
        self.tt(POOL, tmp, xw[:, :, 0:512], xw[:, :, 2:514], ALU.add, [txw], [ttmp])
        self.stt(xx, tmp, 0.5, xw[:, :, 1:513], ALU.mult, ALU.subtract, [ttmp, txw], [txx])
        for p in range(6):
            for c in range(8):
                self.stt(xm[p][0][:, c, :], xx[:, c, :], mu[:, p * 8 + c:p * 8 + c + 1], xw[:, c, 1:513],
                         ALU.mult, ALU.add, [txx, tmu, txw], [xm[p][1]])
        for p, dst in ((0, RR), (1, KK), (2, VV)):
            for tt in range(4):
                y, ty = y_r.next()
                for nh in range(2):
                    ps, tps = self.ps()
                    for c in range(8):
                        self.mm(ps, xm[p][0][:, c, tt * 128:(tt + 1) * 128], wr[p][0][:, c, nh * 512:(nh + 1) * 512],
                                c == 0, c == 7, [xm[p][1], wr[p][1]], [tps])
                    self.copy(self.ev(), y[:, nh * 512:(nh + 1) * 512], ps, [tps], [ty])
                self.dma(dst[t0 + tt * 128:t0 + (tt + 1) * 128, :], y, reads=[ty])
        for k_, (p, fn) in enumerate(((3, AF.Tanh), (4, None), (5, AF.Sigmoid))):
            ps, tps = self.ps()
            for c in range(8):
                self.mm(ps, l1[k_][0][:, c, :], xm[p][0][:, c, :], c == 0, c == 7, [l1[k_][1], xm[p][1]], [tps])
            if fn is None:
                self.copy(ACT, hT[k_][0], ps, [tps], [hT[k_][1]])
            else:
                self.act(hT[k_][0], ps, fn, [tps], [hT[k_][1]])
        for tt in range(4):
            ts_ = slice(tt * 128, (tt + 1) * 128)
            r0 = t0 + tt * 128
            for k_, dsts in ((0, LWP), (1, APR)):
                for n in range(2):
                    y, ty = y_r.next()
                    ns = slice(n * 64, (n + 1) * 64)
                    for nh in range(2):
                        ps, tps = self.ps()
                        self.mm(ps, hT[k_][0][ns, ts_], l2[k_][0][ns, nh * 512:(nh + 1) * 512], True, True,
                                [hT[k_][1], l2[k_][1]], [tps])
                        self.copy(self.ev(), y[:, nh * 512:(nh + 1) * 512], ps, [tps], [ty])
                    self.dma(dsts[n][r0:r0 + 128, :], y, reads=[ty])
            y, ty = y_r.next()
            for nh in range(2):
                ps, tps = self.ps()
                self.mm(ps, hT[2][0][:, ts_], l2[2][0][:, nh * 512:(nh + 1) * 512], True, True, [hT[2][1], l2[2][1]], [tps])
                self.copy(self.ev(), y[:, nh * 512:(nh + 1) * 512], ps, [tps], [ty])
            self.dma(GG[r0:r0 + 128, :], y, reads=[ty])


Prog.stage_rwkv_proj = stage_rwkv_proj


def stage_rwkv_pass(self, n, RR, KK, VV, LWP, APR, GG, OF, BON, MIX):
    self.stage_begin()
    w = self.w
    C = self.cst
    tc_ = self.t_consts
    ident = self.ident
    ones = self.ones
    cf = C("ui") if n == 0 else C("li")
    o_us = CONST_LAYOUT["us"][0]
    o_ls = CONST_LAYOUT["ls"][0]
    m2 = self.consts[:, o_us:o_us + 256] if n == 0 else self.consts[:, o_ls:o_ls + 256]
    m1 = C("ls") if n == 0 else C("us")
    def rowp(ap):
        t_, tk_ = self.tile([128, D])
        self.bcast_row(t_, tk_, ap, D)
        return t_, tk_
    w0, tw0 = rowp(w["od_w0"][n])
    a0, ta0 = rowp(w["od_a0"][n])
    k_k, tkk_ = rowp(w["od_k_k"])
    k_a, tka = rowp(w["od_k_a"])
    r_k, trk = rowp(w["od_r_k"])
    if n == 1:
        lnw, tlnw = rowp(w["od_lnx_w"])
        lnb, tlnb = rowp(w["od_lnx_b"])
    mk = lambda: self.tile([128, D])
    r, tr_ = mk()
    k, tk = mk()
    v, tv = mk()
    lw, tlw = mk()
    a, ta = mk()
    kk, tkk = mk()
    kd, tkd = mk()
    b, tb = mk()
    t1, tt1 = mk()
    t2, tt2 = mk()
    E, tE = mk()
    Rt, tRt = mk()
    KKt, tKKt = mk()
    nBt, tnBt = mk()
    Kt, tKt = mk()
    nBh, tnBh = mk()
    Kh, tKh = mk()
    O, tO = mk()
    st, tst = self.tile([128, 64])
    PL, tPL = self.tile([128, 8])
    nBtT, tnBtT = self.tile([128, 8, 128])
    KtT, tKtT = self.tile([128, 8, 128])
    KR, tKR = self.tile([128, 8, 2, 128])
    ST = [self.tile([128, 64]) for _ in range(8)]
    for s_, ts_ in ST:
        self.memset(DVE, s_, 0.0, [ts_])
    NS = 4
    slots = []
    for s in range(NS):
        d_ = {}
        for nm, shp in (("AB", [128, 256]), ("AK", [128, 256]), ("N0T", [128, 128]), ("TA", [128, 128]), ("TB", [128, 128]),
                        ("NA", [128, 128]), ("NAT", [128, 128]), ("NB", [128, 128]), ("NBT", [128, 128]),
                        ("W1", [128, 64]), ("MU", [128, 128]), ("Q1T", [128, 128]), ("GPT", [128, 64])):
            d_[nm] = self.tile(shp)
        slots.append(d_)

    def head_gen(h, sl):
        c = h // 2
        P = slice((h % 2) * 64, (h % 2) * 64 + 64)
        cb = slice(h * 64, (h + 1) * 64)
        AB, tAB = sl["AB"]
        AK, tAK = sl["AK"]
        N0T, tN0T = sl["N0T"]
        W1, tW1 = sl["W1"]
        MU, tMU = sl["MU"]
        Q1T, tQ1T = sl["Q1T"]
        GPT, tGPT = sl["GPT"]
        Sx, tSx = ST[c]
        krf = KR[P, c, :, :].rearrange("p a t -> p (a t)")
        p1, tp1 = self.ps()
        self.mm(p1[:, 0:256], nBtT[P, c, :], krf, True, True, [tnBtT, tKR], [tp1])
        self.tt(DVE, AB, p1[:, 0:256], m2, ALU.mult, [tp1, tc_], [tAB])
        yield
        p2, tp2 = self.ps()
        self.mm(p2[:, 0:256], KtT[P, c, :], krf, True, True, [tKtT, tKR], [tp2])
        self.tt(DVE, AK, p2[:, 0:256], m2, ALU.mult, [tp2, tc_], [tAK])
        yield
        p3, tp3 = self.ps()
        self.mm(p3[:, 0:128], KR[P, c, 0, :], nBtT[P, c, :], True, True, [tKR, tnBtT], [tp3])
        self.tt(DVE, N0T, p3[:, 0:128], m1, ALU.mult, [tp3, tc_], [tN0T])
        yield
        Tc, tTc = sl["TA"]
        To, tTo = sl["TB"]
        self.tt(POOL, Tc, AB[:, 0:128], ident, ALU.add, [tAB, tc_], [tTc])
        Na, tNa = AB[:, 0:128], tAB
        NaT, tNaT = N0T, tN0T
        bufs = [(sl["NA"], sl["NAT"]), (sl["NB"], sl["NBT"])]
        for lev in range(1, 7):
            (Nb, tNb), (NbT, tNbT) = bufs[lev % 2]
            if lev < 6:
                pn, tpn = self.ps()
                self.mm(pn[:, 0:128], NaT, Na, True, True, [tNaT, tNa], [tpn])
                self.copy(ACT, Nb, pn[:, 0:128], [tpn], [tNb])
            pnt, tpnt = self.ps()
            self.mm(pnt[:, 0:128], Na, NaT, True, True, [tNa, tNaT], [tpnt])
            self.copy(ACT, NbT, pnt[:, 0:128], [tpnt], [tNbT])
            yield
            pt_, tpt_ = self.ps()
            self.mm(pt_[:, 0:128], NbT, Tc, True, True, [tNbT, tTc], [tpt_])
            self.tt(DVE, To, pt_[:, 0:128], Tc, ALU.add, [tpt_, tTc], [tTo])
            (Tc, tTc), (To, tTo) = (To, tTo), (Tc, tTc)
            Na, tNa, NaT, tNaT = Nb, tNb, NbT, tNbT
            yield
        BrbT = AB[:, 128:256]
        AakT = AK[:, 0:128]
        BrkT = AK[:, 128:256]
        pw, tpw = self.ps()
        self.mm(pw[:, 0:64], AakT, v[:, cb], True, True, [tAK, tv], [tpw])
        self.copy(ACT, W1, pw[:, 0:64], [tpw], [tW1])
        yield
        pm, tpm = self.ps()
        self.mm(pm[:, 0:64], Tc, KKt[:, cb], True, True, [tTc, tKKt], [tpm])
        self.mm(pm[:, 64:128], Tc, W1, True, True, [tTc, tW1], [tpm])
        self.copy(ACT, MU, pm[:, 0:128], [tpm], [tMU])
        yield
        pq, tpq = self.ps()
        self.mm(pq[P, 0:128], MU[:, 0:64], BrbT, True, True, [tMU, tAB], [tpq])
        self.tt(DVE, Q1T[P, :], pq[P, 0:128], KR[P, c, 1, :], ALU.add, [tpq, tKR], [tQ1T])
        pg, tpg = self.ps()
        self.mm(pg[P, 0:64], MU[:, 0:64], nBh[:, cb], True, True, [tMU, tnBh], [tpg])
        self.copy(ACT, GPT[P, :], pg[P, 0:64], [tpg], [tGPT])
        yield
        py, tpy = self.ps()
        self.mm(py[:, 0:64], BrbT, MU[:, 64:128], True, False, [tAB, tMU], [tpy])
        self.mm(py[:, 0:64], BrkT, v[:, cb], False, False, [tAK, tv], [tpy])
        self.mm(py[:, 0:64], Q1T[P, :], Sx[P, :], False, True, [tQ1T, tSx], [tpy])
        self.copy(ACT, O[:, cb], py[:, 0:64], [tpy], [tO])
        ph, tph = self.ps()
        self.mm(ph[P, 0:64], nBh[:, cb], MU[:, 64:128], True, False, [tnBh, tMU], [tph])
        self.mm(ph[P, 0:64], Kh[:, cb], v[:, cb], False, False, [tKh, tv], [tph])
        self.mm(ph[P, 0:64], GPT[P, :], Sx[P, :], False, True, [tGPT, tSx], [tph])
        self.stt(Sx[P, :], Sx[P, :], PL[P, c:c + 1], ph[P, 0:64], ALU.mult, ALU.add, [tSx, tPL, tph], [tSx])
        yield

    h16 = lambda ap: ap.rearrange("p (h m) -> p h m", h=16)
    b16 = lambda ap: ap.rearrange("p (h o) -> p h o", o=1).to_broadcast([128, 16, 64])
    order = list(range(NTILE)) if n == 0 else list(reversed(range(NTILE)))
    if DBG_TILES:
        order = order[:DBG_TILES]
    for i in order:
        u = i // 16
        rows = slice(i * 128, (i + 1) * 128)
        if (n == 0 and i % 16 == 0) or (n == 1 and i % 16 == 15):
            fc = u if n == 0 else 3 + u
            for s_, ts_ in ST:
                self.ts(DVE, s_, s_, self.flags[:, fc:fc + 1], ALU.mult, [ts_, self.t_flags], [ts_])
        self.dma(r, RR[rows, :], writes=[tr_])
        self.dma(k, KK[rows, :], writes=[tk])
        self.dma(v, VV[rows, :], writes=[tv])
        self.dma(lw, LWP[n][rows, :], writes=[tlw])
        self.dma(a, APR[n][rows, :], writes=[ta])
        self.tt(POOL, lw, lw, w0, ALU.add, [tlw, tw0], [tlw])
        self.act(lw, lw, AF.Sigmoid, [tlw], [tlw])
        self.ts(DVE, lw, lw, -0.6065306597126334, ALU.mult, [tlw], [tlw])
        self.tt(POOL, a, a, a0, ALU.add, [ta, ta0], [ta])
        self.act(a, a, AF.Sigmoid, [ta], [ta])
        self.tt(POOL, t1, k, k_k, ALU.mult, [tk, tkk_], [tt1])
        self.tt(POOL, t2, t1, t1, ALU.mult, [tt1], [tt2])
        self.red(DVE, st[:, 0:16], h16(t2), ALU.add, [tt2], [tst])
        self.act(st[:, 0:16], st[:, 0:16], AF.Sqrt, [tst], [tst])
        self.ts(DVE, st[:, 0:16], st[:, 0:16], 1e-12, ALU.max, [tst], [tst])
        self.recip(st[:, 16:32], st[:, 0:16], [tst], [tst])
        self.tt(DVE, h16(kk), h16(t1), b16(st[:, 16:32]), ALU.mult, [tt1, tst], [tkk])
        self.stt(t1, a, -1.0, k_a, ALU.add, ALU.mult, [ta, tka], [tt1])
        self.tt(POOL, t1, t1, k, ALU.mult, [tt1, tk], [tt1])
        self.tt(POOL, kd, t1, k, ALU.add, [tt1, tk], [tkd])
        self.tt(POOL, b, kk, a, ALU.mult, [tkk, ta], [tb])
        self.tt(POOL, t2, r, kd, ALU.mult, [tr_, tkd], [tt2])
        self.tt(POOL, t2, t2, r_k, ALU.mult, [tt2, trk], [tt2])
        self.red(DVE, st[:, 32:48], h16(t2), ALU.add, [tt2], [tst])
        self.tt(DVE, h16(t2), h16(v), b16(st[:, 32:48]), ALU.mult, [tv, tst], [tt2])
        if n == 0:
            self.dma(BON[rows, :], t2, reads=[tt2])
        pcs = []
        pts = []
        for nh in range(2):
            pc, tpc = self.ps()
            self.mm(pc, cf, lw[:, nh * 512:(nh + 1) * 512], True, True, [tc_, tlw], [tpc])
            pcs.append((pc, tpc))
        for nh in range(2):
            pt, tpt = self.ps()
            self.mm(pt, ones, lw[:, nh * 512:(nh + 1) * 512], True, True, [tc_, tlw], [tpt])
            pts.append((pt, tpt))
        ppl, tppl = self.ps()
        for c in range(8):
            self.mm(ppl[:, c:c + 1], lw[:, c * 128:(c + 1) * 128], ones[:, 0:1], True, True, [tlw, tc_], [tppl])
        self.act(PL, ppl[:, 0:8], AF.Exp, [tppl], [tPL])
        for nh in range(2):
            hs = slice(nh * 512, (nh + 1) * 512)
            pc, tpc = pcs[nh]
            self.act(E[:, hs], pc, AF.Exp, [tpc], [tE])
        self.tt(DVE, Rt, r, E, ALU.mult, [tr_, tE], [tRt])
        for nh in range(2):
            hs = slice(nh * 512, (nh + 1) * 512)
            pc, tpc = pcs[nh]
            self.tt(DVE, t1[:, hs], pc, lw[:, hs], ALU.subtract, [tpc, tlw], [tt1])
        self.act(t1, t1, AF.Exp, [tt1], [tt1])
        self.tt(POOL, KKt, kk, t1, ALU.mult, [tkk, tt1], [tKKt])
        for nh in range(2):
            hs = slice(nh * 512, (nh + 1) * 512)
            pc, tpc = pcs[nh]
            self.act(E[:, hs], pc, AF.Exp, [tpc], [tE], scale=-1.0)
        self.stt(nBt, b, -1.0, E, ALU.mult, ALU.mult, [tb, tE], [tnBt])
        self.tt(POOL, Kt, kd, E, ALU.mult, [tkd, tE], [tKt])
        for nh in range(2):
            hs = slice(nh * 512, (nh + 1) * 512)
            pt, tpt = pts[nh]
            self.act(t1[:, hs], pt, AF.Exp, [tpt], [tt1])
        self.tt(POOL, nBh, nBt, t1, ALU.mult, [tnBt, tt1], [tnBh])
        self.tt(POOL, Kh, Kt, t1, ALU.mult, [tKt, tt1], [tKh])
        for (src, tsrc, dst, tdst) in ((nBt, tnBt, nBtT, tnBtT), (Kt, tKt, KtT, tKtT)):
            for g in range(2):
                ps, tps = self.ps()
                for cc in range(4):
                    c = g * 4 + cc
                    self.tr(ps[:, cc * 128:(cc + 1) * 128], src[:, c * 128:(c + 1) * 128], [tsrc], [tps])
                self.copy(self.ev(), dst[:, g * 4:(g + 1) * 4, :], ps.rearrange("p (c t) -> p c t", c=4), [tps], [tdst])
        for g in range(4):
            ps, tps = self.ps()
            for cc in range(2):
                c = g * 2 + cc
                self.tr(ps[:, cc * 256:cc * 256 + 128], KKt[:, c * 128:(c + 1) * 128], [tKKt], [tps])
                self.tr(ps[:, cc * 256 + 128:cc * 256 + 256], Rt[:, c * 128:(c + 1) * 128], [tRt], [tps])
            self.copy(self.ev(), KR[:, g * 2:(g + 1) * 2, :, :], ps.rearrange("p (c a t) -> p c a t", c=2, a=2), [tps], [tKR])
        for hg in range(0, 16, NS):
            gens = [head_gen(hg + s, slots[s]) for s in range(NS)]
            alive = list(gens)
            while alive:
                nxt = []
                for g_ in alive:
                    try:
                        next(g_)
                        nxt.append(g_)
                    except StopIteration:
                        pass
                alive = nxt
        if n == 0:
            self.dma(OF[rows, :], O, reads=[tO])
        else:
            self.dma(t1, OF[rows, :], writes=[tt1])
            self.tt(DVE, O, O, t1, ALU.add, [tO, tt1], [tO])
            self.dma(Rt, BON[rows, :], writes=[tRt])
            self.dma(KKt, GG[rows, :], writes=[tKKt])
            self.red(DVE, st[:, 0:16], h16(O), ALU.add, [tO], [tst])
            self.ts(DVE, st[:, 0:16], st[:, 0:16], 1.0 / 64.0, ALU.mult, [tst], [tst])
            self.tt(DVE, h16(O), h16(O), b16(st[:, 0:16]), ALU.subtract, [tO, tst], [tO])
            self.tt(POOL, t1, O, O, ALU.mult, [tO], [tt1])
            self.red(DVE, st[:, 16:32], h16(t1), ALU.add, [tt1], [tst])
            self.ts(DVE, st[:, 16:32], st[:, 16:32], 1.0 / 64.0, ALU.mult, [tst], [tst], s2=64e-5, op1=ALU.add)
            self.act(st[:, 16:32], st[:, 16:32], AF.Sqrt, [tst], [tst])
            self.recip(st[:, 48:64], st[:, 16:32], [tst], [tst])
            self.tt(DVE, h16(O), h16(O), b16(st[:, 48:64]), ALU.mult, [tO, tst], [tO])
            self.tt(POOL, O, O, lnw, ALU.mult, [tO, tlnw], [tO])
            self.tt(POOL, O, O, lnb, ALU.add, [tO, tlnb], [tO])
            self.tt(POOL, O, O, t2, ALU.add, [tO, tt2], [tO])
            self.tt(POOL, O, O, Rt, ALU.add, [tO, tRt], [tO])
            self.tt(DVE, O, O, KKt, ALU.mult, [tO, tKKt], [tO])
            self.dma(MIX[rows, :], O, reads=[tO])


Prog.stage_rwkv_pass = stage_rwkv_pass


def odd_mixer(self):
    mkd = lambda nm: self.dr(nm, [T, D])
    RR, KK, VV, GG = mkd("RR"), mkd("KK"), mkd("VV"), mkd("GG")
    LWP = [mkd("LWP0"), mkd("LWP1")]
    APR = [mkd("APR0"), mkd("APR1")]
    OF, BON, MIX = mkd("OF"), mkd("BON"), mkd("MIX")
    self.stage_rwkv_proj(RR, KK, VV, LWP, APR, GG)
    self.stage_rwkv_pass(0, RR, KK, VV, LWP, APR, GG, OF, BON, MIX)
    self.stage_rwkv_pass(1, RR, KK, VV, LWP, APR, GG, OF, BON, MIX)
    XT2 = self.dr("XT2", [D, T], BF16)
    self.stage_ln(MIX, None, None, do_ln=False, xt_out=XT2)
    self.stage_linear(self.w["od_w_out"], self.H, D, xt=XT2)


Prog.odd_mixer = odd_mixer


def odd_weights(inp, j):
    return {
        "od_mu_t": np.ascontiguousarray(inp["od_mu"][j].reshape(6, 8, 128).transpose(2, 0, 1).reshape(128, 48)),
        "od_w_rkv": inp["od_w_rkv"][j],
        "od_w1c": np.ascontiguousarray(inp["od_w1"][j].transpose(1, 0, 2).reshape(D, 128)),
        "od_a1c": np.ascontiguousarray(inp["od_a1"][j].transpose(1, 0, 2).reshape(D, 128)),
        "od_g1": inp["od_g1"][j],
        "od_w2c": np.ascontiguousarray(inp["od_w2"][j].reshape(128, D)),
        "od_a2c": np.ascontiguousarray(inp["od_a2"][j].reshape(128, D)),
        "od_g2": inp["od_g2"][j],
        "od_w0": inp["od_w0"][j], "od_a0": inp["od_a0"][j], "od_k_k": inp["od_k_k"][j], "od_k_a": inp["od_k_a"][j],
        "od_r_k": inp["od_r_k"][j], "od_lnx_w": inp["od_lnx_w"][j], "od_lnx_b": inp["od_lnx_b"][j],
        "od_w_out": inp["od_w_out"][j],
    }


ODD_SHAPES = {"od_mu_t": [128, 48], "od_w_rkv": [3, D, D], "od_w1c": [D, 128], "od_a1c": [D, 128], "od_g1": [D, 128],
              "od_w2c": [128, D], "od_a2c": [128, D], "od_g2": [128, D], "od_w0": [2, D], "od_a0": [2, D],
              "od_k_k": [D], "od_k_a": [D], "od_r_k": [D], "od_lnx_w": [D], "od_lnx_b": [D], "od_w_out": [D, D]}


EVEN_SHAPES = {"ev_w_in": [D, 4608], "ev_lb_logits": [2, 2, 512], "ev_norm_a": [512], "ev_norm_b": [512],
               "ev_w_out": [D, D], "rope": [T, 192]}
COMMON_MIX = {"x": [T, D], "mem": [NU * MEM, D], "consts": [128, NCONST], "flags": [128, 16], "ln_w": [3, D], "ln_b": [3, D],
              "ca_w_q": [D, D], "ca_w_kv": [D, 2 * D], "ca_w_out": [D, D], "moe_router_t": [NE, D]}
MOE_INS = {"x": [T, D], "aff_loc": [T, NE], "aff_all": [NCORES * T, NE], "consts": [128, NCONST], "flags": [128, 16],
           "moe_w_in": [NE, D, 2 * DEXP], "moe_w_out": [NE, DEXP, D], "ln_w": [3, D], "ln_b": [3, D]}


def build_mix(kind, j):
    ins = dict(COMMON_MIX)
    if kind == "even":
        ins.update(EVEN_SHAPES)
    elif kind == "odd":
        ins.update(ODD_SHAPES)
    P = Prog(ins, {"x2": [T, D], "aff": [T, NE]})
    xin = P.i["x"]
    P.stage_ln(xin, None, None, do_ln=False)
    if kind == "even":
        P.even_mixer(j)
    elif kind == "odd":
        P.odd_mixer()
    if kind is not None:
        P.stage_ln(xin, P.H, P.X[0], lnw=P.i["ln_w"][0], lnb=P.i["ln_b"][0])
        xcur = P.X[0]
    else:
        xcur = xin
    P.stage_ln(P.i["mem"], None, None, do_ln=False, xt_out=P.MT, ntok=NU * MEM)
    P.stage_xattn()
    P.stage_ln(xcur, P.H, P.o["x2"], lnw=P.i["ln_w"][1], lnb=P.i["ln_b"][1], router_w=P.i["moe_router_t"])
    return P


def build_moe():
    P = Prog(dict(MOE_INS), {"x3": [T, D]})
    P.stage_ln(P.i["x"], None, None, do_ln=False)
    P.stage_topk()
    P.stage_moe()
    P.stage_ln(P.i["x"], P.H, P.o["x3"], lnw=P.i["ln_w"][2], lnb=P.i["ln_b"][2])
    return P


def split_cores(inp):
    xs, ms = [], []
    for c in range(NCORES):
        xl, ml = [], []
        for (g, b, half) in core_units(c):
            if g == "p":
                xl.append(inp["x_prompt"][b, half * TU:(half + 1) * TU])
                ml.append(inp["mem_prompt"][b])
            else:
                xl.append(inp["x_sample"][b])
                ml.append(inp["mem_sample"][b])
        xs.append(np.ascontiguousarray(np.concatenate(xl, 0)))
        ms.append(np.ascontiguousarray(np.concatenate(ml, 0)))
    return xs, ms


def run_mix(kind, layer, inp, xs, ms):
    j = layer // 2
    P = build_mix(kind, j)
    nc = P.finish()
    common = {"consts": make_consts(), "ln_w": inp["ln_w"][layer], "ln_b": inp["ln_b"][layer],
              "ca_w_q": inp["ca_w_q"][layer], "ca_w_kv": inp["ca_w_kv"][layer], "ca_w_out": inp["ca_w_out"][layer],
              "moe_router_t": np.ascontiguousarray(inp["moe_router"][layer].T)}
    if kind == "even":
        common.update({"ev_w_in": inp["ev_w_in"][j], "ev_lb_logits": inp["ev_lb_logits"], "ev_norm_a": inp["ev_norm_a"][j],
                       "ev_norm_b": inp["ev_norm_b"][j], "ev_w_out": inp["ev_w_out"][j]})
    elif kind == "odd":
        common.update(odd_weights(inp, j))
    maps = []
    for c in range(NCORES):
        m = dict(common)
        m["x"] = xs[c]
        m["mem"] = ms[c]
        m["flags"] = make_flags(c)
        if kind == "even":
            m["rope"] = make_rope(c)
        maps.append(m)
    res = run_bass_kernel_spmd(nc, maps, core_ids=list(range(NCORES)))
    return [r["x2"] for r in res.results], [r["aff"] for r in res.results]


def run_moe(layer, inp, xs, affs):
    P = build_moe()
    nc = P.finish()
    aff_all = np.ascontiguousarray(np.concatenate(affs, 0))
    common = {"consts": make_consts(), "aff_all": aff_all, "moe_w_in": inp["moe_w_in"][layer],
              "moe_w_out": inp["moe_w_out"][layer], "ln_w": inp["ln_w"][layer], "ln_b": inp["ln_b"][layer]}
    maps = []
    for c in range(NCORES):
        m = dict(common)
        m["x"] = xs[c]
        m["aff_loc"] = affs[c]
        m["flags"] = make_flags(c)
        maps.append(m)
    res = run_bass_kernel_spmd(nc, maps, core_ids=list(range(NCORES)))
    return [r["x3"] for r in res.results]


def kernel(**inp):
    inp = {k: np.asarray(v) for k, v in inp.items()}
    xs, ms = split_cores(inp)
    for layer in range(DEPTH):
        kind = "even" if layer % 2 == 0 else "odd"
        xs, affs = run_mix(kind, layer, inp, xs, ms)
        xs = run_moe(layer, inp, xs, affs)
    yp = np.zeros((4, 4096, D), np.float32)
    ysm = np.zeros((16, 2048, D), np.float32)
    for c in range(NCORES):
        for u, (g, b, half) in enumerate(core_units(c)):
            blk = xs[c][u * TU:(u + 1) * TU]
            if g == "p":
                yp[b, half * TU:(half + 1) * TU] = blk
            else:
                ysm[b] = blk
    return (yp, ysm)
```

```python
import numpy as np
import concourse.bass as bass
import concourse.mybir as mybir
from concourse.bass_utils import run_bass_kernel_spmd

F32 = mybir.dt.float32
BF16 = mybir.dt.bfloat16
AF = mybir.ActivationFunctionType
ALU = mybir.AluOpType
AX = mybir.AxisListType
PE, ACT, DVE, POOL, SP = "tensor", "scalar", "vector", "gpsimd", "sync"

NCORES = 8
D = 1024
T = 6144
TU = 2048
NU = 3
NTILE = T // 128
DEPTH = 4
DN_ALPHA = (2.0 * DEPTH) ** 0.25
LN_EPS = 1e-5
NE = 16
DEXP = 2048
CAP_P = 2048
CAP_S = 4096
MEM = 256
SEM_CAP = 30000
DBG_TILES = None
F32T_ENG = None
F32T_SKIP = False
F32T_2D = False
SBW = 51200


class Tok:
    __slots__ = ("w", "r")

    def __init__(self):
        self.w = None
        self.r = {}


class FW:
    def __init__(self, n_dma_sems=16):
        self.nc = bass.Bass("TRN2", target_bir_lowering=False)
        self.q = {e: [] for e in (PE, ACT, DVE, POOL, SP)}
        self.cnt = {e: 0 for e in self.q}
        self.waited = {e: {} for e in self.q}
        self.n_dma_sems = n_dma_sems
        self.dma_val = {}
        self.dma_rr = {e: 0 for e in self.q}
        self.dma_ep = {}
        self.log = []
        self.ctx = []
        self.sems = {}
        self.ninstr = 0

    def enter(self, cm):
        v = cm.__enter__()
        self.ctx.append(cm)
        return v

    def sem(self, key):
        if key not in self.sems:
            nm = "s_" + "_".join(str(k) for k in key)
            self.sems[key] = self.enter(self.nc.semaphore(nm))
        return self.sems[key]

    def _wait(self, eng, key, val):
        if key[0] == eng and eng == PE and len(key) == 2:
            return
        if self.waited[eng].get(key, 0) >= val:
            return
        self.waited[eng][key] = val
        s = self.sem(key)
        self.q[eng].append(lambda e, s=s, val=val: e.wait_ge(s, val))
        self.log.append((eng, 'wait', key, val))

    def _deps(self, eng, reads, writes):
        for t in reads:
            if t.w is not None:
                self._wait(eng, *t.w)
        for t in writes:
            if t.w is not None:
                self._wait(eng, *t.w)
            for k, v in t.r.items():
                self._wait(eng, k, v)

    def _mark(self, key, val, reads, writes):
        for t in reads:
            t.r[key] = val
        for t in writes:
            t.w = (key, val)
            t.r = {}

    def _ekey(self, eng):
        c = self.cnt[eng]
        return (eng, (c - 1) // SEM_CAP), (c - 1) % SEM_CAP + 1

    def op(self, eng, fn, reads=(), writes=()):
        self._deps(eng, reads, writes)
        self.cnt[eng] += 1
        key, val = self._ekey(eng)
        s = self.sem(key)
        self.q[eng].append(lambda e, fn=fn, s=s: fn(e).then_inc(s, 1))
        self.log.append((eng, 'op', key, val))
        self._mark(key, val, reads, writes)
        self.ninstr += 1

    def dma(self, out, in_, reads=(), writes=(), q=SP, fn=None, **kw):
        self._deps(q, reads, writes)
        i = self.dma_rr[q] % self.n_dma_sems
        self.dma_rr[q] += 1
        ep = self.dma_ep.get((q, i), 0)
        key = (q, i, ep)
        prev = self.dma_val.get(key, 0)
        if prev + 16 > SEM_CAP:
            self._wait(q, key, prev)
            ep += 1
            self.dma_ep[(q, i)] = ep
            key = (q, i, ep)
            prev = 0
        if prev:
            self._wait(q, key, prev)
        val = prev + 16
        self.dma_val[key] = val
        s = self.sem(key)
        if fn is None:
            self.q[q].append(lambda e, s=s, out=out, in_=in_, kw=kw:
                             e.dma_start(out=out, in_=in_, **kw).then_inc(s, 16))
        else:
            self.q[q].append(lambda e, s=s, fn=fn: fn(e).then_inc(s, 16))
        self.log.append((q, 'dma', key, val))
        self._mark(key, val, reads, writes)
        self.ninstr += 1

    def barrier(self):
        for eng in self.q:
            for e2 in self.q:
                if e2 != eng and self.cnt[e2]:
                    key, val = self._ekey(e2)
                    self._wait(eng, key, val)
            for key, val in self.dma_val.items():
                self._wait(eng, key, val)

    def finish(self):
        self.barrier()
        nc = self.nc
        with nc.Block() as block:
            for name in (PE, ACT, DVE, POOL, SP):
                lst = self.q[name]

                def body(e, lst=lst):
                    for f in lst:
                        f(e)
                getattr(block, name)(body)
        for cm in reversed(self.ctx):
            cm.__exit__(None, None, None)
        self.ctx = []
        return nc


def _prod(s):
    r = 1
    for v in s:
        r *= v
    return r


class KB(FW):
    def __init__(self):
        super().__init__()
        nc = self.nc
        self.big = self.enter(nc.sbuf_tensor("big", [128, SBW], F32))
        self.off = 0
        self.base = 0
        self.banks = [(self.enter(nc.psum_tensor("ps%d" % i, [128, 512], F32))[:, :], Tok()) for i in range(8)]
        self.bank_i = 0
        self.ev_i = 0

    def tile(self, shape, dt=F32):
        parts = shape[0]
        n = _prod(shape[1:])
        words = n if dt == F32 else (n + 1) // 2
        words = (words + 1) // 2 * 2
        assert self.off + words <= SBW, ("SBUF arena overflow", self.off, words)
        ap = self.big[:, self.off:self.off + words]
        self.off += words
        if dt != F32:
            ap = ap.bitcast(dt)
        ap = ap[:parts, :n]
        if len(shape) > 2:
            names = " ".join("d%d" % i for i in range(len(shape) - 1))
            kw = {"d%d" % i: shape[i + 1] for i in range(len(shape) - 2)}
            ap = ap.rearrange("p (%s) -> p %s" % (names, names), **kw)
        return ap, Tok()

    def ring(self, n, shape, dt=F32):
        return _Ring([self.tile(shape, dt) for _ in range(n)])

    def stage_begin(self):
        self.barrier()
        self.off = self.base

    def persist(self):
        self.base = self.off

    def ps(self):
        r = self.banks[self.bank_i % 8]
        self.bank_i += 1
        return r

    def ev(self):
        self.ev_i += 1
        return ACT if self.ev_i % 2 else DVE

    def copy(self, eng, out, in_, reads, writes, scale=None):
        if eng == ACT:
            if scale is None:
                self.op(ACT, lambda e: e.copy(out=out, in_=in_), reads, writes)
            else:
                self.op(ACT, lambda e: e.mul(out=out, in_=in_, mul=scale), reads, writes)
        else:
            if scale is None:
                self.op(eng, lambda e: e.tensor_copy(out=out, in_=in_), reads, writes)
            else:
                self.op(eng, lambda e: e.tensor_scalar(out=out, in0=in_, scalar1=scale, scalar2=None,
                                                       op0=ALU.mult), reads, writes)


class _Ring:
    def __init__(self, tiles):
        self.tiles = tiles
        self.i = 0

    def next(self):
        r = self.tiles[self.i % len(self.tiles)]
        self.i += 1
        return r


def _build_consts():
    lay = {}
    cols = []
    off = [0]

    def add(name, arr):
        arr = np.asarray(arr, np.float64)
        lay[name] = (off[0], arr.shape[1])
        off[0] += arr.shape[1]
        cols.append(arr)

    idx = np.arange(128)
    s_ = idx[:, None]
    t_ = idx[None, :]
    same = (s_ // 64) == (t_ // 64)
    add("ident", np.eye(128))
    add("ones", np.ones((128, 128)))
    add("tri_f", same & (s_ <= t_))
    add("tri_b", same & (s_ >= t_))
    add("blk", same)
    ind = np.zeros((128, 8))
    ind[:64, 0] = 1
    ind[64:, 1] = 1
    add("ind", ind)
    add("us", s_ < t_)
    add("ui", s_ <= t_)
    add("ls", s_ > t_)
    add("li", s_ >= t_)
    gam = 1.0 - 2.0 ** (-5.0 - np.arange(4))
    for d in range(2):
        for h in range(4):
            if d == 0:
                m = np.where(s_ <= t_, gam[h] ** np.maximum(t_ - s_, 0), 0.0)
            else:
                m = np.where(s_ > t_, gam[h] ** np.maximum(s_ - t_, 0), 0.0)
            add("rmask%d%d" % (d, h), m)
    for d in range(2):
        for h in range(4):
            row = gam[h] ** (idx + 1.0) if d == 0 else gam[h] ** (128.0 - idx)
            add("rqd%d%d" % (d, h), np.broadcast_to(row[None, :], (128, 128)))
    kd = np.zeros((128, 8))
    for d in range(2):
        for h in range(4):
            kd[:, d * 4 + h] = gam[h] ** (127.0 - idx) if d == 0 else gam[h] ** (idx * 1.0)
    add("rkd", kd)
    c = np.concatenate(cols, axis=1).astype(np.float32)
    return lay, c, gam


CONST_LAYOUT, _CONSTS, _GAM = _build_consts()
NCONST = _CONSTS.shape[1]
C_IDENT = CONST_LAYOUT["ident"][0]
C_ONES = CONST_LAYOUT["ones"][0]


def make_consts():
    return _CONSTS


def make_rope(c):
    pos = np.concatenate([np.arange(2048) + (2048 if (c < 4 and u == 1) else 0) for u in range(NU)]).astype(np.float32)
    theta = (1.0 / np.power(np.float32(10000.0), np.linspace(0.0, 1.0, 64, dtype=np.float32))).astype(np.float32)
    ang = pos[:, None] * theta[None, :]
    out = np.zeros((T, 192), np.float32)
    out[:, 0:128] = np.repeat(np.cos(ang), 2, axis=1)
    out[:, 128:192] = np.sin(ang)
    return out


def make_flags(c):
    fl = np.zeros((128, 16), np.float32)
    if c < 4:
        fl[:, 1] = 1
        fl[:, 3] = 1
        fl[:, 6] = 1
        fl[:, 7] = 1
    fl[:, 9:12] = 1 - fl[:, 6:9]
    return fl


def core_units(c):
    if c < 4:
        return [("p", c, 0), ("p", c, 1), ("s", c, 0)]
    b = 4 + 3 * (c - 4)
    return [("s", b, 0), ("s", b + 1, 0), ("s", b + 2, 0)]


class Prog(KB):
    def __init__(self, ins, outs, dbg=None):
        super().__init__()
        self.dbg = dbg or {}
        nc = self.nc
        self.i = {}
        for name, shape in ins.items():
            self.i[name] = nc.dram_tensor(name, list(shape), F32, kind="ExternalInput").ap()
        self.o = {}
        for name, shape in outs.items():
            self.o[name] = nc.dram_tensor(name, list(shape), F32, kind="ExternalOutput").ap()
        self.w = self.i
        self._dr = {}
        self.X = [self.dr("Xa", [T, D]), self.dr("Xb", [T, D])]
        self.H = self.dr("H", [T, D])
        self.XT = self.dr("XT", [D, T], BF16)
        self.XTf = self.dr("XTf", [D, T])
        self.MT = self.dr("MT", [D, NU * MEM], BF16)
        self.dbg_out = {}
        for name, shape in self.dbg.items():
            self.dbg_out[name] = nc.dram_tensor("dbg_" + name, list(shape), F32, kind="ExternalOutput").ap()

        self.consts, self.t_consts = self.tile([128, NCONST])
        self.flags, self.t_flags = self.tile([128, 16])
        self.AFF, self.t_aff = self.tile([128, NTILE, NE])
        self.G, self.t_g = self.tile([128, NTILE, NE])
        self.persist()
        self.dma(self.consts, self.i["consts"], writes=[self.t_consts])
        self.dma(self.flags, self.i["flags"], writes=[self.t_flags])
        self.ident = self.consts[:, C_IDENT:C_IDENT + 128]
        self.ones = self.consts[:, C_ONES:C_ONES + 128]

    def dr(self, name, shape, dt=F32):
        if name not in self._dr:
            self._dr[name] = self.nc.dram_tensor(name, list(shape), dt, kind="Internal").ap()
        return self._dr[name]

    def cst(self, name):
        o, n = CONST_LAYOUT[name]
        return self.consts[:, o:o + n]

    def load_w_bf16(self, dst, tok, w_ap):
        wv = w_ap.rearrange("(c p) n -> p c n", p=128)
        for c in range(wv.shape[1]):
            self.dma(dst[:, c, :], wv[:, c, :], writes=[tok], q=POOL)

    def bcast_row(self, dst, tok, row_ap, n):
        self.dma(dst, row_ap.partition_broadcast(128), writes=[tok])

    def stage_ln(self, Xold, Hs, Xnew, lnw=None, lnb=None, do_ln=True, router_w=None, want_f32T=False,
                 xt_out=None, ntok=T):
        self.stage_begin()
        xt_out = self.XT if xt_out is None else xt_out
        ntile = ntok // 128
        xin = self.ring(2, [128, D])
        hin = self.ring(2, [128, D])
        st_r = self.ring(2, [128, 2, 6])
        mv_r = self.ring(2, [128, 4])
        xtb = self.ring(2, [128, 8, 512], BF16)
        if do_ln:
            wrep, t_w = self.tile([128, D])
            brep, t_b = self.tile([128, D])
            self.bcast_row(wrep, t_w, lnw, D)
            self.bcast_row(brep, t_b, lnb, D)
        if router_w is not None:
            wr, t_wr = self.tile([128, NE, D])
            for ex_i in range(NE):
                self.dma(wr[:, ex_i, :], router_w[ex_i].partition_broadcast(128), writes=[t_wr])
            lg_r = self.ring(2, [128, NE])
            rt_r = self.ring(2, [128, D])
            sm_r = self.ring(2, [128, 4])
            ex_r = self.ring(2, [128, NE])
        xT_blk = None
        for i in range(ntile):
            rows = slice(i * 128, (i + 1) * 128)
            x, tx = xin.next()
            self.dma(x, Xold[rows, :], writes=[tx])
            if do_ln:
                h, th = hin.next()
                self.dma(h, Hs[rows, :], writes=[th])
                self.op(DVE, lambda e, x=x, h=h: e.scalar_tensor_tensor(
                    out=x, in0=x, scalar=DN_ALPHA, in1=h, op0=ALU.mult, op1=ALU.add), [tx, th], [tx])
                st, tst = st_r.next()
                for k in range(2):
                    self.op(DVE, lambda e, st=st, x=x, k=k: e.bn_stats(out=st[:, k, :], in_=x[:, k * 512:(k + 1) * 512]),
                            [tx], [tst])
                mv, tmv = mv_r.next()
                self.op(DVE, lambda e, mv=mv, st=st: e.bn_aggr(out=mv[:, 0:2], in_=st.rearrange("p a b -> p (a b)")),
                        [tst], [tmv])
                self.op(DVE, lambda e, mv=mv: e.tensor_scalar(out=mv[:, 2:3], in0=mv[:, 1:2], scalar1=LN_EPS,
                                                             scalar2=None, op0=ALU.add), [tmv], [tmv])
                self.op(ACT, lambda e, mv=mv: e.activation(out=mv[:, 2:3], in_=mv[:, 2:3], func=AF.Sqrt), [tmv], [tmv])
                self.op(DVE, lambda e, mv=mv: e.reciprocal(out=mv[:, 3:4], in_=mv[:, 2:3]), [tmv], [tmv])
                self.op(DVE, lambda e, mv=mv, x=x: e.tensor_scalar(out=x, in0=x, scalar1=mv[:, 0:1], scalar2=mv[:, 3:4],
                                                                  op0=ALU.subtract, op1=ALU.mult), [tx, tmv], [tx])
                self.op(POOL, lambda e, x=x: e.tensor_tensor(out=x, in0=x, in1=wrep, op=ALU.mult), [tx, t_w], [tx])
                self.op(POOL, lambda e, x=x: e.tensor_tensor(out=x, in0=x, in1=brep, op=ALU.add), [tx, t_b], [tx])
                self.dma(Xnew[rows, :], x, reads=[tx])
            if i % 4 == 0:
                xT_blk, t_blk = xtb.next()
            for half in range(2):
                ps, tps = self.ps()
                for k in range(4):
                    c = half * 4 + k
                    self.op(PE, lambda e, ps=ps, x=x, k=k, c=c: e.transpose(
                        out=ps[:, k * 128:(k + 1) * 128], in_=x[:, c * 128:(c + 1) * 128], identity=self.ident),
                        [tx, self.t_consts], [tps])
                dst = xT_blk[:, half * 4:(half + 1) * 4, (i % 4) * 128:(i % 4 + 1) * 128]
                src = ps.rearrange("p (c t) -> p c t", c=4)
                self.copy(ACT, dst, src, [tps], [t_blk])
            if i % 4 == 3 or i == ntile - 1:
                t0 = (i // 4) * 512
                wd = (i % 4 + 1) * 128
                self.dma(xt_out.rearrange("(c p) t -> p c t", p=128)[:, :, t0:t0 + wd], xT_blk[:, :, 0:wd], reads=[t_blk])
            if router_w is not None:
                lg, tlg = lg_r.next()
                for ex_i in range(NE):
                    tmp, ttmp = rt_r.next()
                    self.op(POOL, lambda e, tmp=tmp, x=x, ex_i=ex_i: e.tensor_tensor(
                        out=tmp, in0=x, in1=wr[:, ex_i, :], op=ALU.mult), [tx, t_wr], [ttmp])
                    self.op(DVE, lambda e, tmp=tmp, lg=lg, ex_i=ex_i: e.tensor_reduce(
                        out=lg[:, ex_i:ex_i + 1], in_=tmp, axis=AX.X, op=ALU.add), [ttmp], [tlg])
                ps, tps = lg, tlg
                sm, tsm = sm_r.next()
                ex, tex = ex_r.next()
                self.op(DVE, lambda e, sm=sm, ps=ps: e.tensor_reduce(out=sm[:, 0:1], in_=ps[:, 0:NE], axis=AX.X, op=ALU.max),
                        [tps], [tsm])
                self.op(DVE, lambda e, sm=sm: e.tensor_scalar(out=sm[:, 1:2], in0=sm[:, 0:1], scalar1=-1.0, scalar2=None,
                                                             op0=ALU.mult), [tsm], [tsm])
                self.op(ACT, lambda e, sm=sm, ex=ex, ps=ps: e.activation(out=ex, in_=ps[:, 0:NE], func=AF.Exp,
                                                                         bias=sm[:, 1:2], scale=1.0), [tps, tsm], [tex])
                self.op(DVE, lambda e, sm=sm, ex=ex: e.tensor_reduce(out=sm[:, 2:3], in_=ex, axis=AX.X, op=ALU.add),
                        [tex], [tsm])
                self.op(DVE, lambda e, sm=sm: e.reciprocal(out=sm[:, 3:4], in_=sm[:, 2:3]), [tsm], [tsm])
                self.op(DVE, lambda e, sm=sm, ex=ex, i=i: e.tensor_scalar(out=self.AFF[:, i, :], in0=ex, scalar1=sm[:, 3:4],
                                                                         scalar2=None, op0=ALU.mult),
                        [tex, tsm], [self.t_aff])
        if router_w is not None:
            self.dma(self.o["aff"].rearrange("(i p) e -> p i e", p=128), self.AFF, reads=[self.t_aff])

    def stage_linear(self, W, Y, N, ntok=T, xt=None):
        self.stage_begin()
        xt = self.XT if xt is None else xt
        wt, t_wt = self.tile([128, 8, N], BF16)
        self.load_w_bf16(wt, t_wt, W)
        xr = self.ring(2, [128, 8, 512], BF16)
        yr = self.ring(2, [128, N])
        for tb in range(ntok // 512):
            xb, txb = xr.next()
            self.dma(xb, xt.rearrange("(c p) t -> p c t", p=128)[:, :, tb * 512:(tb + 1) * 512], writes=[txb])
            for tt in range(4):
                y, ty = yr.next()
                for n0 in range(0, N, 512):
                    nw = min(512, N - n0)
                    ps, tps = self.ps()
                    for c in range(8):
                        self.op(PE, lambda e, ps=ps, xb=xb, c=c, tt=tt, n0=n0, nw=nw: e.matmul(
                            ps[:, 0:nw], lhsT=xb[:, c, tt * 128:(tt + 1) * 128], rhs=wt[:, c, n0:n0 + nw],
                            start=(c == 0), stop=(c == 7)), [txb, t_wt], [tps])
                    self.copy(self.ev(), y[:, n0:n0 + nw], ps[:, 0:nw], [tps], [ty])
                r0 = tb * 512 + tt * 128
                self.dma(Y[r0:r0 + 128, :], y, reads=[ty])

    def stage_xattn(self):
        self.stage_begin()
        wq, t_wq = self.tile([128, 8, D], BF16)
        wkv, t_wkv = self.tile([128, 8, 2 * D], BF16)
        wo, t_wo = self.tile([128, 8, D], BF16)
        self.load_w_bf16(wq, t_wq, self.w["ca_w_q"])
        self.load_w_bf16(wkv, t_wkv, self.w["ca_w_kv"])
        self.load_w_bf16(wo, t_wo, self.w["ca_w_out"])
        ones_bf, t_ob = self.tile([128, 128], BF16)
        self.copy(DVE, ones_bf, self.ones, [self.t_consts], [t_ob])
        mt, t_mt = self.tile([128, 8, MEM], BF16)
        KT, t_KT = self.tile([128, 8, MEM], BF16)
        V, t_V = self.tile([128, 2, D], BF16)
        xr = self.ring(2, [128, 8, 512], BF16)
        qT, t_qT = self.tile([128, 8, 512], BF16)
        E_r = self.ring(2, [128, 2, 512], BF16)
        R_r = self.ring(2, [128, 512])
        oT, t_oT = self.tile([128, 8, 512], BF16)
        hr = self.ring(2, [128, D])
        for u in range(NU):
            self.dma(mt, self.MT.rearrange("(c p) t -> p c t", p=128)[:, :, u * MEM:(u + 1) * MEM], writes=[t_mt])
            for oc in range(8):
                ps, tps = self.ps()
                for c in range(8):
                    self.op(PE, lambda e, ps=ps, c=c, oc=oc: e.matmul(ps[:, 0:MEM], lhsT=wkv[:, c, oc * 128:(oc + 1) * 128],
                                                                      rhs=mt[:, c, :], start=(c == 0), stop=(c == 7)),
                            [t_wkv, t_mt], [tps])
                self.copy(self.ev(), KT[:, oc, :], ps[:, 0:MEM], [tps], [t_KT])
            for mc in range(2):
                for dh in range(2):
                    ps, tps = self.ps()
                    for c in range(8):
                        self.op(PE, lambda e, ps=ps, c=c, mc=mc, dh=dh: e.matmul(
                            ps, lhsT=mt[:, c, mc * 128:(mc + 1) * 128], rhs=wkv[:, c, D + dh * 512:D + (dh + 1) * 512],
                            start=(c == 0), stop=(c == 7)), [t_wkv, t_mt], [tps])
                    self.copy(self.ev(), V[:, mc, dh * 512:(dh + 1) * 512], ps, [tps], [t_V])
            for tbu in range(TU // 512):
                tb = u * (TU // 512) + tbu
                xb, txb = xr.next()
                self.dma(xb, self.XT.rearrange("(c p) t -> p c t", p=128)[:, :, tb * 512:(tb + 1) * 512], writes=[txb])
                for oc in range(8):
                    ps, tps = self.ps()
                    for c in range(8):
                        self.op(PE, lambda e, ps=ps, c=c, oc=oc, xb=xb: e.matmul(
                            ps, lhsT=wq[:, c, oc * 128:(oc + 1) * 128], rhs=xb[:, c, :], start=(c == 0), stop=(c == 7)),
                            [t_wq, txb], [tps])
                    self.copy(self.ev(), qT[:, oc, :], ps, [tps], [t_qT], scale=1.0 / 16.0)
                for hh in range(4):
                    E, tE = E_r.next()
                    for mc in range(2):
                        ps, tps = self.ps()
                        for j in range(2):
                            self.op(PE, lambda e, ps=ps, mc=mc, j=j, hh=hh: e.matmul(
                                ps, lhsT=KT[:, hh * 2 + j, mc * 128:(mc + 1) * 128], rhs=qT[:, hh * 2 + j, :],
                                start=(j == 0), stop=(j == 1)), [t_KT, t_qT], [tps])
                        self.op(ACT, lambda e, ps=ps, E=E, mc=mc: e.activation(out=E[:, mc, :], in_=ps, func=AF.Exp),
                                [tps], [tE])
                    psd, tpsd = self.ps()
                    for mc in range(2):
                        self.op(PE, lambda e, psd=psd, E=E, mc=mc: e.matmul(psd, lhsT=ones_bf, rhs=E[:, mc, :],
                                                                            start=(mc == 0), stop=(mc == 1)),
                                [t_ob, tE], [tpsd])
                    R, tR = R_r.next()
                    self.op(DVE, lambda e, R=R, psd=psd: e.reciprocal(out=R, in_=psd), [tpsd], [tR])
                    for j in range(2):
                        ps, tps = self.ps()
                        for mc in range(2):
                            self.op(PE, lambda e, ps=ps, E=E, mc=mc, j=j, hh=hh: e.matmul(
                                ps, lhsT=V[:, mc, hh * 256 + j * 128:hh * 256 + (j + 1) * 128], rhs=E[:, mc, :],
                                start=(mc == 0), stop=(mc == 1)), [t_V, tE], [tps])
                        self.op(DVE, lambda e, ps=ps, R=R, j=j, hh=hh: e.tensor_tensor(
                            out=oT[:, hh * 2 + j, :], in0=ps, in1=R, op=ALU.mult), [tps, tR], [t_oT])
                for tt in range(4):
                    h, th = hr.next()
                    for dh in range(2):
                        ps, tps = self.ps()
                        for c in range(8):
                            self.op(PE, lambda e, ps=ps, c=c, tt=tt, dh=dh: e.matmul(
                                ps, lhsT=oT[:, c, tt * 128:(tt + 1) * 128], rhs=wo[:, c, dh * 512:(dh + 1) * 512],
                                start=(c == 0), stop=(c == 7)), [t_oT, t_wo], [tps])
                        self.copy(self.ev(), h[:, dh * 512:(dh + 1) * 512], ps, [tps], [th])
                    r0 = tb * 512 + tt * 128
                    self.dma(self.H[r0:r0 + 128, :], h, reads=[th])

    def stage_topk(self):
        self.stage_begin()
        aff_all = self.i["aff_all"]
        self.dma(self.AFF, self.i["aff_loc"].rearrange("(i p) e -> p i e", p=128), writes=[self.t_aff])
        JP, JS = 128, 256
        A, tA = self.tile([128, JP + JS, NE])
        for c in range(4):
            self.dma(A[:, c * 32:(c + 1) * 32, :],
                     aff_all[c * T:c * T + 4096, :].rearrange("(p j) e -> p j e", j=32), writes=[tA])
            self.dma(A[:, JP + c * 16:JP + (c + 1) * 16, :],
                     aff_all[c * T + 4096:(c + 1) * T, :].rearrange("(p j) e -> p j e", j=16), writes=[tA])
        for c in range(4, 8):
            o = JP + 64 + (c - 4) * 48
            self.dma(A[:, o:o + 48, :], aff_all[c * T:(c + 1) * T, :].rearrange("(p j) e -> p j e", j=48),
                     writes=[tA])
        cmp_, tcmp = self.tile([128, JP + JS, NE])
        lo, tlo = self.tile([128, 2, NE])
        hi, thi = self.tile([128, 2, NE])
        mid, tmid = self.tile([128, 2, NE])
        cnt, tcnt = self.tile([128, 2, NE])
        capt, tcap = self.tile([128, 2, NE])
        m, tm = self.tile([128, 2, NE])
        t1, tt1 = self.tile([128, 2, NE])
        self.op(DVE, lambda e: e.memset(lo, 0.0), [], [tlo])
        self.op(DVE, lambda e: e.memset(hi, 2.0), [], [thi])
        self.op(DVE, lambda e: e.memset(mid, 1.0), [], [tmid])
        self.op(DVE, lambda e: e.memset(capt[:, 0, :], float(CAP_P)), [], [tcap])
        self.op(DVE, lambda e: e.memset(capt[:, 1, :], float(CAP_S)), [], [tcap])
        segs = [(0, 0, JP), (1, JP, JS)]
        for it in range(40):
            for g, j0, jn in segs:
                self.op(DVE, lambda e, g=g, j0=j0, jn=jn: e.tensor_tensor(
                    out=cmp_[:, j0:j0 + jn, :], in0=A[:, j0:j0 + jn, :],
                    in1=mid[:, g:g + 1, :].to_broadcast([128, jn, NE]), op=ALU.is_ge), [tA, tmid], [tcmp])
                self.op(DVE, lambda e, g=g, j0=j0, jn=jn: e.tensor_reduce(
                    out=cnt[:, g, :], in_=cmp_[:, j0:j0 + jn, :].rearrange("p j e -> p e j"), axis=AX.X, op=ALU.add),
                    [tcmp], [tcnt])
            ps, tps = self.ps()
            self.op(PE, lambda e, ps=ps: e.matmul(ps[:, 0:2 * NE], lhsT=self.ones, rhs=cnt.rearrange("p g e -> p (g e)"),
                                                  start=True, stop=True), [tcnt, self.t_consts], [tps])
            self.op(DVE, lambda e, ps=ps: e.tensor_tensor(out=m.rearrange("p g e -> p (g e)"), in0=ps[:, 0:2 * NE],
                                                          in1=capt.rearrange("p g e -> p (g e)"), op=ALU.is_ge),
                    [tps, tcap], [tm])
            self.op(DVE, lambda e: e.tensor_tensor(out=t1, in0=m, in1=mid, op=ALU.mult), [tm, tmid], [tt1])
            self.op(DVE, lambda e: e.tensor_tensor(out=lo, in0=lo, in1=t1, op=ALU.max), [tlo, tt1], [tlo])
            self.op(DVE, lambda e: e.scalar_tensor_tensor(out=t1, in0=m, scalar=4.0, in1=mid, op0=ALU.mult, op1=ALU.add),
                    [tm, tmid], [tt1])
            self.op(DVE, lambda e: e.tensor_tensor(out=hi, in0=hi, in1=t1, op=ALU.min), [thi, tt1], [thi])
            self.op(DVE, lambda e: e.tensor_tensor(out=mid, in0=lo, in1=hi, op=ALU.add), [tlo, thi], [tmid])
            self.op(DVE, lambda e: e.tensor_scalar(out=mid, in0=mid, scalar1=0.5, scalar2=None, op0=ALU.mult),
                    [tmid], [tmid])
        gp, tgp = self.tile([128, 16, NE])
        gs, tgs = self.tile([128, 16, NE])
        for u in range(NU):
            Au = self.AFF[:, u * 16:(u + 1) * 16, :]
            for g, gt, tgt in ((0, gp, tgp), (1, gs, tgs)):
                self.op(DVE, lambda e, g=g, gt=gt, Au=Au: e.tensor_tensor(
                    out=gt, in0=Au, in1=lo[:, g:g + 1, :].to_broadcast([128, 16, NE]), op=ALU.is_ge),
                    [self.t_aff, tlo], [tgt])
                self.op(DVE, lambda e, gt=gt, Au=Au: e.tensor_tensor(out=gt, in0=gt, in1=Au, op=ALU.mult),
                        [self.t_aff, tgt], [tgt])
            fp = self.flags[:, 6 + u:7 + u]
            nfp = self.flags[:, 9 + u:10 + u]
            self.op(DVE, lambda e, fp=fp: e.tensor_scalar(out=gp, in0=gp, scalar1=fp, scalar2=None, op0=ALU.mult),
                    [tgp, self.t_flags], [tgp])
            self.op(DVE, lambda e, nfp=nfp, u=u: e.scalar_tensor_tensor(
                out=self.G[:, u * 16:(u + 1) * 16, :], in0=gs, scalar=nfp, in1=gp, op0=ALU.mult, op1=ALU.add),
                [tgs, tgp, self.t_flags], [self.t_g])
        if "thr" in self.dbg_out:
            self.dma(self.dbg_out["thr"], lo.rearrange("p g e -> p (g e)"), reads=[tlo])

    def stage_moe(self):
        self.stage_begin()
        w_in = self.w["moe_w_in"]
        w_out = self.w["moe_w_out"]
        xT, t_xT = self.tile([128, 8, TU], BF16)
        acc, t_acc = self.tile([128, 16, D])
        wi_r = self.ring(2, [128, 8, 1024], BF16)
        wo_r = self.ring(2, [128, 4, D], BF16)
        s_r = self.ring(2, [128, 512])
        h_r = self.ring(2, [128, 4, 512], BF16)
        for u in range(NU):
            self.dma(xT, self.XT.rearrange("(c p) t -> p c t", p=128)[:, :, u * TU:(u + 1) * TU], writes=[t_xT])
            for ex in range(NE):
                for qd in range(4):
                    f0 = qd * 512
                    wi, twi = wi_r.next()
                    wo, two = wo_r.next()
                    self.dma(wi[:, :, 0:512], w_in[ex][:, f0:f0 + 512].rearrange("(c p) n -> p c n", p=128),
                             writes=[twi], q=POOL)
                    self.dma(wi[:, :, 512:1024],
                             w_in[ex][:, DEXP + f0:DEXP + f0 + 512].rearrange("(c p) n -> p c n", p=128),
                             writes=[twi], q=POOL)
                    self.dma(wo, w_out[ex][f0:f0 + 512, :].rearrange("(c p) n -> p c n", p=128), writes=[two], q=POOL)
                    first = (ex == 0 and qd == 0)
                    for tb in range(TU // 512):
                        hT, thT = h_r.next()
                        for fc in range(4):
                            psg, tpsg = self.ps()
                            psu, tpsu = self.ps()
                            for c in range(8):
                                self.op(PE, lambda e, psg=psg, wi=wi, c=c, fc=fc, tb=tb: e.matmul(
                                    psg, lhsT=wi[:, c, fc * 128:(fc + 1) * 128], rhs=xT[:, c, tb * 512:(tb + 1) * 512],
                                    start=(c == 0), stop=(c == 7)), [twi, t_xT], [tpsg])
                            for c in range(8):
                                self.op(PE, lambda e, psu=psu, wi=wi, c=c, fc=fc, tb=tb: e.matmul(
                                    psu, lhsT=wi[:, c, 512 + fc * 128:512 + (fc + 1) * 128],
                                    rhs=xT[:, c, tb * 512:(tb + 1) * 512], start=(c == 0), stop=(c == 7)),
                                    [twi, t_xT], [tpsu])
                            s, ts = s_r.next()
                            self.op(ACT, lambda e, s=s, psg=psg: e.activation(out=s, in_=psg, func=AF.Silu), [tpsg], [ts])
                            self.op(DVE, lambda e, s=s, psu=psu, hT=hT, fc=fc: e.tensor_tensor(
                                out=hT[:, fc, :], in0=psu, in1=s, op=ALU.mult), [tpsu, ts], [thT])
                        for tt in range(4):
                            ti = tb * 4 + tt
                            gcol = self.G[:, u * 16 + ti, ex:ex + 1]
                            for dh in range(2):
                                ps, tps = self.ps()
                                for fc in range(4):
                                    self.op(PE, lambda e, ps=ps, hT=hT, wo=wo, fc=fc, tt=tt, dh=dh: e.matmul(
                                        ps, lhsT=hT[:, fc, tt * 128:(tt + 1) * 128], rhs=wo[:, fc, dh * 512:(dh + 1) * 512],
                                        start=(fc == 0), stop=(fc == 3)), [thT, two], [tps])
                                dst = acc[:, ti, dh * 512:(dh + 1) * 512]
                                if first:
                                    self.op(DVE, lambda e, ps=ps, dst=dst, gcol=gcol: e.tensor_scalar(
                                        out=dst, in0=ps, scalar1=gcol, scalar2=None, op0=ALU.mult),
                                        [tps, self.t_g], [t_acc])
                                else:
                                    self.op(DVE, lambda e, ps=ps, dst=dst, gcol=gcol: e.scalar_tensor_tensor(
                                        out=dst, in0=ps, scalar=gcol, in1=dst, op0=ALU.mult, op1=ALU.add),
                                        [tps, self.t_g, t_acc], [t_acc])
            for ti in range(16):
                r0 = u * TU + ti * 128
                self.dma(self.H[r0:r0 + 128, :], acc[:, ti, :], reads=[t_acc])

    def dump(self, name, src):
        if name in self.dbg_out:
            self.barrier()
            self.dma(self.dbg_out[name], src)


def _tt(self, eng, out, in0, in1, op, R, W):
    self.op(eng, lambda e: e.tensor_tensor(out=out, in0=in0, in1=in1, op=op), R, W)


def _ts(self, eng, out, in0, s1, op0, R, W, s2=None, op1=None):
    if op1 is None:
        self.op(eng, lambda e: e.tensor_scalar(out=out, in0=in0, scalar1=s1, scalar2=None, op0=op0), R, W)
    else:
        self.op(eng, lambda e: e.tensor_scalar(out=out, in0=in0, scalar1=s1, scalar2=s2, op0=op0, op1=op1), R, W)


def _stt(self, out, in0, scalar, in1, op0, op1, R, W):
    self.op(DVE, lambda e: e.scalar_tensor_tensor(out=out, in0=in0, scalar=scalar, in1=in1, op0=op0, op1=op1), R, W)


def _act(self, out, in_, func, R, W, scale=1.0, bias=None):
    if bias is None:
        self.op(ACT, lambda e: e.activation(out=out, in_=in_, func=func, scale=scale), R, W)
    else:
        self.op(ACT, lambda e: e.activation(out=out, in_=in_, func=func, scale=scale, bias=bias), R, W)


def _mm(self, ps, lhsT, rhs, start, stop, R, W):
    self.op(PE, lambda e: e.matmul(ps, lhsT=lhsT, rhs=rhs, start=start, stop=stop), R, W)


def _tr(self, ps, in_, R, W):
    ident = self.ident
    n = in_.shape[0]
    self.op(PE, lambda e: e.transpose(out=ps, in_=in_, identity=ident[:n, :n]), list(R) + [self.t_consts], W)


def _red(self, eng, out, in_, op, R, W):
    self.op(eng, lambda e: e.tensor_reduce(out=out, in_=in_, axis=AX.X, op=op), R, W)


def _recip(self, out, in_, R, W):
    self.op(DVE, lambda e: e.reciprocal(out=out, in_=in_), R, W)


def _memset(self, eng, ap, val, W):
    self.op(eng, lambda e: e.memset(ap, val), [], W)


for _f in (_tt, _ts, _stt, _act, _mm, _tr, _red, _recip, _memset):
    setattr(Prog, _f.__name__[1:], _f)


def _b3(ap, n):
    m = ap.shape[1]
    return ap.rearrange("p (o m) -> p o m", o=1).to_broadcast([128, n, m])


def _v3(ap, n):
    return ap.rearrange("p (h m) -> p h m", h=n)


def stage_hgrn(self, j, d, PROJ, OF, MIX):
    self.stage_begin()
    C = self.cst
    tri = C("tri_f") if d == 0 else C("tri_b")
    blk = C("blk")
    ind = C("ind")
    tc_ = self.t_consts
    lbl = self.w["ev_lb_logits"]
    if j == 1:
        l0, tl0 = self.tile([128, 512])
        lbr, tlb = self.tile([128, 512])
        oml, tom = self.tile([128, 512])
        self.bcast_row(l0, tl0, lbl[d, 0], 512)
        self.bcast_row(lbr, tlb, lbl[d, 1], 512)
        self.tt(DVE, lbr, lbr, l0, ALU.subtract, [tlb, tl0], [tlb])
        self.act(lbr, lbr, AF.Sigmoid, [tlb], [tlb])
        self.ts(DVE, oml, lbr, -1.0, ALU.mult, [tlb], [tom], s2=1.0, op1=ALU.add)
    if d == 1:
        nrm, tnrm = self.tile([128, 512])
        self.bcast_row(nrm, tnrm, self.w["ev_norm_a"], 512)
    aq_r = self.ring(2, [128, 512])
    ai_r = self.ring(2, [128, 512])
    z_r = self.ring(2, [128, 512])
    sg_r = self.ring(2, [128, 512])
    lf_r = self.ring(2, [128, 512])
    kin_r = self.ring(2, [128, 512])
    qin_r = self.ring(2, [128, 512])
    e_r = self.ring(3, [128, 512])
    qt_r = self.ring(2, [128, 512])
    kt_r = self.ring(2, [128, 512])
    kh_r = self.ring(2, [128, 512], BF16)
    vb_r = self.ring(2, [128, 512], BF16)
    pl_r = self.ring(2, [128, 8])
    qT_r = self.ring(2, [128, 4, 128], BF16)
    kT_r = self.ring(2, [128, 4, 128], BF16)
    sc_r = self.ring(3, [128, 128], BF16)
    o_r = self.ring(2, [128, 512])
    if d == 1:
        ag_r = self.ring(2, [128, 512])
        of_r = self.ring(2, [128, 512])
        sq_r = self.ring(2, [128, 512])
        st_r = self.ring(2, [128, 8])
    S, tS = self.tile([128, 4, 128])
    Sb, tSb = self.tile([128, 4, 128], BF16)
    self.memset(DVE, S, 0.0, [tS])
    self.memset(DVE, Sb, 0.0, [tSb])
    order = list(range(NTILE)) if d == 0 else list(reversed(range(NTILE)))
    for i in order:
        u = i // 16
        rows = slice(i * 128, (i + 1) * 128)
        if (d == 0 and i % 16 == 0) or (d == 1 and i % 16 == 15):
            fcol = self.flags[:, (u if d == 0 else 3 + u):(u if d == 0 else 3 + u) + 1]
            self.ts(DVE, S, S, fcol, ALU.mult, [tS, self.t_flags], [tS])
            self.copy(ACT, Sb, S, [tS], [tSb])
        aq, taq = aq_r.next()
        ai, tai = ai_r.next()
        z, tz = z_r.next()
        self.dma(aq, PROJ[rows, 0:512], writes=[taq])
        self.dma(ai, PROJ[rows, 512:1024], writes=[tai])
        self.dma(z, PROJ[rows, 1024 + d * 512:1536 + d * 512], writes=[tz])
        sg, tsg = sg_r.next()
        lf, tlf = lf_r.next()
        kin, tkin = kin_r.next()
        qin, tqin = qin_r.next()
        self.act(sg, z, AF.Sigmoid, [tz], [tsg])
        self.ts(DVE, kin, sg, -1.0, ALU.mult, [tsg], [tkin], s2=1.0, op1=ALU.add)
        if j == 1:
            self.tt(DVE, kin, kin, oml, ALU.mult, [tkin, tom], [tkin])
            self.tt(DVE, sg, sg, oml, ALU.mult, [tsg, tom], [tsg])
            self.tt(DVE, sg, sg, lbr, ALU.add, [tsg, tlb], [tsg])
        self.act(lf, sg, AF.Ln, [tsg], [tlf])
        self.act(qin, aq, AF.Silu, [taq], [tqin])
        pc, tpc = self.ps()
        self.mm(pc, tri, lf, True, True, [tc_, tlf], [tpc])
        pt, tpt = self.ps()
        self.mm(pt, blk, lf, True, True, [tc_, tlf], [tpt])
        pp, tpp = self.ps()
        for h in range(4):
            self.mm(pp[:, 2 * h:2 * h + 2], lf[:, h * 128:(h + 1) * 128], ind[:, 0:2], True, True, [tlf, tc_], [tpp])
        pl, tpl = pl_r.next()
        self.act(pl, pp[:, 0:8], AF.Exp, [tpp], [tpl])
        e1, te1 = e_r.next()
        self.act(e1, pc, AF.Exp, [tpc], [te1])
        qt, tqt = qt_r.next()
        self.tt(DVE, qt, qin, e1, ALU.mult, [tqin, te1], [tqt])
        e2, te2 = e_r.next()
        self.act(e2, pc, AF.Exp, [tpc], [te2], scale=-1.0)
        kt, tkt = kt_r.next()
        self.tt(DVE, kt, kin, e2, ALU.mult, [tkin, te2], [tkt])
        e3, te3 = e_r.next()
        self.act(e3, pt, AF.Exp, [tpt], [te3])
        kh, tkh = kh_r.next()
        self.tt(DVE, kh, kt, e3, ALU.mult, [tkt, te3], [tkh])
        vb, tvb = vb_r.next()
        self.copy(ACT, vb, ai, [tai], [tvb])
        qT, tqT = qT_r.next()
        kT, tkT = kT_r.next()
        for (src, tsrc, dst, tdst) in ((qt, tqt, qT, tqT), (kt, tkt, kT, tkT)):
            ps, tps = self.ps()
            for h in range(4):
                self.tr(ps[:, h * 128:(h + 1) * 128], src[:, h * 128:(h + 1) * 128], [tsrc], [tps])
            self.copy(ACT, dst, ps.rearrange("p (h t) -> p h t", h=4), [tps], [tdst])
        o, to = o_r.next()
        for h in range(4):
            hs = slice(h * 128, (h + 1) * 128)
            psc, tpsc = self.ps()
            self.mm(psc[:, 0:128], kT[:, h, :], qT[:, h, :], True, True, [tkT, tqT], [tpsc])
            scm, tscm = sc_r.next()
            self.tt(DVE, scm, psc[:, 0:128], tri, ALU.mult, [tpsc, tc_], [tscm])
            py, tpy = self.ps()
            self.mm(py[:, 0:128], scm, vb[:, hs], True, False, [tscm, tvb], [tpy])
            cs = (0, 1) if d == 0 else (1, 0)
            for ci, c in enumerate(cs):
                cr = slice(c * 64, (c + 1) * 64)
                self.mm(py[cr, 0:128], qT[:, h, cr], Sb[:, h, :], False, ci == 1, [tqT, tSb], [tpy])
                pn, tpn = self.ps()
                self.mm(pn[:, 0:128], kh[cr, hs], vb[cr, hs], True, True, [tkh, tvb], [tpn])
                self.stt(S[:, h, :], S[:, h, :], pl[:, 2 * h + c:2 * h + c + 1], pn[:, 0:128], ALU.mult, ALU.add,
                         [tS, tpl, tpn], [tS])
                self.copy(ACT, Sb[:, h, :], S[:, h, :], [tS], [tSb])
            self.copy(ACT, o[:, hs], py[:, 0:128], [tpy], [to])
        if d == 0:
            self.dma(OF[rows, 0:512], o, reads=[to])
        else:
            of, tof = of_r.next()
            ag, tag = ag_r.next()
            self.dma(of, OF[rows, 0:512], writes=[tof])
            self.dma(ag, PROJ[rows, 2048:2560], writes=[tag])
            self.tt(DVE, o, o, of, ALU.add, [to, tof], [to])
            sq, tsq = sq_r.next()
            st, tst = st_r.next()
            self.tt(POOL, sq, o, o, ALU.mult, [to], [tsq])
            self.red(DVE, st[:, 0:4], _v3(sq, 4), ALU.add, [tsq], [tst])
            self.ts(DVE, st[:, 0:4], st[:, 0:4], 1.0 / 128.0, ALU.mult, [tst], [tst], s2=1e-6, op1=ALU.add)
            self.act(st[:, 0:4], st[:, 0:4], AF.Sqrt, [tst], [tst])
            self.recip(st[:, 4:8], st[:, 0:4], [tst], [tst])
            self.tt(DVE, _v3(o, 4), _v3(o, 4), st[:, 4:8].rearrange("p (h o) -> p h o", o=1).to_broadcast([128, 4, 128]),
                    ALU.mult, [to, tst], [to])
            self.tt(POOL, o, o, nrm, ALU.mult, [to, tnrm], [to])
            self.act(ag, ag, AF.Silu, [tag], [tag])
            self.tt(DVE, o, o, ag, ALU.mult, [to, tag], [to])
            self.dma(MIX[rows, 0:512], o, reads=[to])


Prog.stage_hgrn = stage_hgrn


def stage_ret(self, d, PROJ, OF, MIX):
    self.stage_begin()
    C = self.cst
    tc_ = self.t_consts
    rope = self.i["rope"]
    o0 = CONST_LAYOUT["rmask%d0" % d][0]
    maskT = self.consts[:, o0:o0 + 512].rearrange("p (h t) -> p h t", h=4)
    q0 = CONST_LAYOUT["rqd%d0" % d][0]
    qdr = self.consts[:, q0:q0 + 512].rearrange("p (h t) -> p h t", h=4)
    k0 = CONST_LAYOUT["rkd"][0] + d * 4
    kdc = self.consts[:, k0:k0 + 4]
    if d == 1:
        nrm, tnrm = self.tile([128, 512])
        self.bcast_row(nrm, tnrm, self.w["ev_norm_b"], 512)
    q_r = self.ring(2, [128, 512])
    k_r = self.ring(2, [128, 512])
    v_r = self.ring(2, [128, 512])
    rp_r = self.ring(2, [128, 192])
    qr_r = self.ring(2, [128, 512])
    kr_r = self.ring(2, [128, 512])
    tmp_r = self.ring(2, [128, 512])
    kd_r = self.ring(2, [128, 512], BF16)
    vb_r = self.ring(2, [128, 512], BF16)
    qT_r = self.ring(2, [128, 4, 128], BF16)
    qdT_r = self.ring(2, [128, 4, 128], BF16)
    kT_r = self.ring(2, [128, 4, 128], BF16)
    sc_r = self.ring(3, [128, 128], BF16)
    o_r = self.ring(2, [128, 512])
    if d == 1:
        bg_r = self.ring(2, [128, 512])
        of_r = self.ring(2, [128, 512])
        sq_r = self.ring(2, [128, 512])
        st_r = self.ring(2, [128, 12])
    R, tR = self.tile([128, 4, 128])
    Rb, tRb = self.tile([128, 4, 128], BF16)
    self.memset(DVE, R, 0.0, [tR])
    self.memset(DVE, Rb, 0.0, [tRb])
    KSC = 128.0 ** -0.5
    order = list(range(NTILE)) if d == 0 else list(reversed(range(NTILE)))
    for i in order:
        u = i // 16
        rows = slice(i * 128, (i + 1) * 128)
        if (d == 0 and i % 16 == 0) or (d == 1 and i % 16 == 15):
            fc = u if d == 0 else 3 + u
            self.ts(DVE, R, R, self.flags[:, fc:fc + 1], ALU.mult, [tR, self.t_flags], [tR])
            self.copy(ACT, Rb, R, [tR], [tRb])
        q, tq = q_r.next()
        k, tk = k_r.next()
        v, tv = v_r.next()
        rp, trp = rp_r.next()
        self.dma(q, PROJ[rows, 2560:3072], writes=[tq])
        self.dma(k, PROJ[rows, 3072:3584], writes=[tk])
        self.dma(v, PROJ[rows, 3584:4096], writes=[tv])
        self.dma(rp, rope[rows, :], writes=[trp])
        cosb = _b3(rp[:, 0:128], 4)
        sinb = rp[:, 128:192].rearrange("p (o m) -> p o m", o=1).to_broadcast([128, 4, 64])
        outs = []
        for (src, tsrc, ring) in ((q, tq, qr_r), (k, tk, kr_r)):
            dst, tdst = ring.next()
            tmp, ttmp = tmp_r.next()
            sv = src.rearrange("p (h i two) -> p h i two", h=4, two=2)
            dv = dst.rearrange("p (h i two) -> p h i two", h=4, two=2)
            self.stt(dv[:, :, :, 0], sv[:, :, :, 1], -1.0, sinb, ALU.mult, ALU.mult, [tsrc, trp], [tdst])
            self.tt(DVE, dv[:, :, :, 1], sv[:, :, :, 0], sinb, ALU.mult, [tsrc, trp], [tdst])
            self.tt(POOL, _v3(tmp, 4), _v3(src, 4), cosb, ALU.mult, [tsrc, trp], [ttmp])
            self.tt(DVE, dst, dst, tmp, ALU.add, [tdst, ttmp], [tdst])
            outs.append((dst, tdst))
        (qr, tqr), (kr, tkr) = outs
        kd, tkd = kd_r.next()
        self.stt(_v3(kd, 4), _v3(kr, 4), KSC, kdc.rearrange("p (h o) -> p h o", o=1).to_broadcast([128, 4, 128]),
                 ALU.mult, ALU.mult, [tkr, tc_], [tkd])
        vb, tvb = vb_r.next()
        self.copy(ACT, vb, v, [tv], [tvb])
        qT, tqT = qT_r.next()
        kT, tkT = kT_r.next()
        qdT, tqdT = qdT_r.next()
        ps, tps = self.ps()
        for h in range(4):
            self.tr(ps[:, h * 128:(h + 1) * 128], qr[:, h * 128:(h + 1) * 128], [tqr], [tps])
        self.copy(ACT, qT, ps.rearrange("p (h t) -> p h t", h=4), [tps], [tqT])
        self.tt(DVE, qdT, ps.rearrange("p (h t) -> p h t", h=4), qdr, ALU.mult, [tps, tc_], [tqdT])
        ps, tps = self.ps()
        for h in range(4):
            self.tr(ps[:, h * 128:(h + 1) * 128], kr[:, h * 128:(h + 1) * 128], [tkr], [tps])
        self.copy(ACT, kT, ps.rearrange("p (h t) -> p h t", h=4), [tps], [tkT], scale=KSC)
        o, to = o_r.next()
        for h in range(4):
            hs = slice(h * 128, (h + 1) * 128)
            psc, tpsc = self.ps()
            self.mm(psc[:, 0:128], kT[:, h, :], qT[:, h, :], True, True, [tkT, tqT], [tpsc])
            scm, tscm = sc_r.next()
            self.tt(DVE, scm, psc[:, 0:128], maskT[:, h, :], ALU.mult, [tpsc, tc_], [tscm])
            py, tpy = self.ps()
            self.mm(py[:, 0:128], scm, vb[:, hs], True, False, [tscm, tvb], [tpy])
            self.mm(py[:, 0:128], qdT[:, h, :], Rb[:, h, :], False, True, [tqdT, tRb], [tpy])
            pn, tpn = self.ps()
            self.mm(pn[:, 0:128], kd[:, hs], vb[:, hs], True, True, [tkd, tvb], [tpn])
            self.stt(R[:, h, :], R[:, h, :], float(_GAM[h] ** 128.0), pn[:, 0:128], ALU.mult, ALU.add, [tR, tpn], [tR])
            self.copy(ACT, Rb[:, h, :], R[:, h, :], [tR], [tRb])
            self.copy(ACT, o[:, hs], py[:, 0:128], [tpy], [to])
        if d == 0:
            self.dma(OF[rows, 512:1024], o, reads=[to])
        else:
            of, tof = of_r.next()
            bg, tbg = bg_r.next()
            self.dma(of, OF[rows, 512:1024], writes=[tof])
            self.dma(bg, PROJ[rows, 4096:4608], writes=[tbg])
            self.tt(DVE, o, o, of, ALU.add, [to, tof], [to])
            sq, tsq = sq_r.next()
            st, tst = st_r.next()
            self.red(DVE, st[:, 8:12], _v3(o, 4), ALU.add, [to], [tst])
            self.ts(DVE, st[:, 8:12], st[:, 8:12], 1.0 / 128.0, ALU.mult, [tst], [tst])
            self.tt(DVE, _v3(o, 4), _v3(o, 4), st[:, 8:12].rearrange("p (h o) -> p h o", o=1).to_broadcast([128, 4, 128]),
                    ALU.subtract, [to, tst], [to])
            self.tt(POOL, sq, o, o, ALU.mult, [to], [tsq])
            self.red(DVE, st[:, 0:4], _v3(sq, 4), ALU.add, [tsq], [tst])
            self.ts(DVE, st[:, 0:4], st[:, 0:4], 1.0 / 128.0, ALU.mult, [tst], [tst], s2=1e-6, op1=ALU.add)
            self.act(st[:, 0:4], st[:, 0:4], AF.Sqrt, [tst], [tst])
            self.recip(st[:, 4:8], st[:, 0:4], [tst], [tst])
            self.tt(DVE, _v3(o, 4), _v3(o, 4), st[:, 4:8].rearrange("p (h o) -> p h o", o=1).to_broadcast([128, 4, 128]),
                    ALU.mult, [to, tst], [to])
            self.tt(POOL, o, o, nrm, ALU.mult, [to, tnrm], [to])
            self.act(bg, bg, AF.Silu, [tbg], [tbg])
            self.tt(DVE, o, o, bg, ALU.mult, [to, tbg], [to])
            self.dma(MIX[rows, 512:1024], o, reads=[to])


Prog.stage_ret = stage_ret


def even_mixer(self, j):
    PROJ = self.dr("PROJ", [T, 4608])
    OF = self.dr("OF", [T, D])
    MIX = self.dr("MIX", [T, D])
    self.stage_linear(self.w["ev_w_in"], PROJ, 4608)
    self.stage_hgrn(j, 0, PROJ, OF, MIX)
    self.stage_ret(0, PROJ, OF, MIX)
    self.stage_hgrn(j, 1, PROJ, OF, MIX)
    self.stage_ret(1, PROJ, OF, MIX)
    XT2 = self.dr("XT2", [D, T], BF16)
    self.stage_ln(MIX, None, None, do_ln=False, xt_out=XT2)
    self.stage_linear(self.w["ev_w_out"], self.H, D, xt=XT2)


Prog.even_mixer = even_mixer


def stage_rwkv_proj(self, RR, KK, VV, LWP, APR, GG):
    self.stage_begin()
    w = self.w
    XTfv = self.XT.rearrange("(c p) t -> p c t", p=128)
    mu, tmu = self.tile([128, 48])
    self.dma(mu, w["od_mu_t"], writes=[tmu])
    wr = []
    for p in range(3):
        wt, twt = self.tile([128, 8, D], BF16)
        self.load_w_bf16(wt, twt, w["od_w_rkv"][p])
        wr.append((wt, twt))
    l1 = []
    for nm in ("od_w1c", "od_a1c", "od_g1"):
        wt, twt = self.tile([128, 8, 128], BF16)
        self.load_w_bf16(wt, twt, w[nm])
        l1.append((wt, twt))
    l2 = []
    for nm in ("od_w2c", "od_a2c", "od_g2"):
        wt, twt = self.tile([128, D], BF16)
        self.dma(wt, w[nm], writes=[twt], q=POOL)
        l2.append((wt, twt))
    xw, txw = self.tile([128, 8, 514], BF16)
    tmp, ttmp = self.tile([128, 8, 512])
    xx, txx = self.tile([128, 8, 512])
    xm = [self.tile([128, 8, 512], BF16) for _ in range(6)]
    hT = [self.tile([128, 512], BF16) for _ in range(3)]
    y_r = self.ring(2, [128, D])
    nblk = T // 512
    for tb in range(nblk):
        t0 = tb * 512
        lo = max(t0 - 1, 0)
        hi = min(t0 + 513, T)
        if tb == 0:
            self.memset(DVE, xw[:, :, 0:1], 0.0, [txw])
        if tb == nblk - 1:
            self.memset(DVE, xw[:, :, 513:514], 0.0, [txw])
        self.dma(xw[:, :, lo - (t0 - 1):hi - (t0 - 1)], XTfv[:, :, lo:hi], writes=[txw])
        if t0 % TU == 0 and tb > 0:
            u = t0 // TU
            self.ts(DVE, xw[:, :, 0:1], xw[:, :, 0:1], self.flags[:, u:u + 1], ALU.mult, [txw, self.t_flags], [txw])
        if (t0 + 512) % TU == 0 and tb < nblk - 1:
            u = t0 // TU
            self.ts(DVE, xw[:, :, 513:514], xw[:, :, 513:514], self.flags[:, 3 + u:4 + u], ALU.mult,
                    [txw, self.t_flags], [txw])
        self.tt(POOL, tmp, xw[:, :, 0:512], xw[:, :, 2:514], ALU.add, [txw], [ttmp])
        self.stt(xx, tmp, 0.5, xw[:, :, 1:513], ALU.mult, ALU.subtract, [ttmp, txw], [txx])
        for p in range(6):
            for c in range(8):
                self.stt(xm[p][0][:, c, :], xx[:, c, :], mu[:, p * 8 + c:p * 8 + c + 1], xw[:, c, 1:513],
                         ALU.mult, ALU.add, [txx, tmu, txw], [xm[p][1]])
        for p, dst in ((0, RR), (1, KK), (2, VV)):
            for tt in range(4):
                y, ty = y_r.next()
                for nh in range(2):
                    ps, tps = self.ps()
                    for c in range(8):
                        self.mm(ps, xm[p][0][:, c, tt * 128:(tt + 1) * 128], wr[p][0][:, c, nh * 512:(nh + 1) * 512],
                                c == 0, c == 7, [xm[p][1], wr[p][1]], [tps])
                    self.copy(self.ev(), y[:, nh * 512:(nh + 1) * 512], ps, [tps], [ty])
                self.dma(dst[t0 + tt * 128:t0 + (tt + 1) * 128, :], y, reads=[ty])
        for k_, (p, fn) in enumerate(((3, AF.Tanh), (4, None), (5, AF.Sigmoid))):
            ps, tps = self.ps()
            for c in range(8):
                self.mm(ps, l1[k_][0][:, c, :], xm[p][0][:, c, :], c == 0, c == 7, [l1[k_][1], xm[p][1]], [tps])
            if fn is None:
                self.copy(ACT, hT[k_][0], ps, [tps], [hT[k_][1]])
            else:
                self.act(hT[k_][0], ps, fn, [tps], [hT[k_][1]])
        for tt in range(4):
            ts_ = slice(tt * 128, (tt + 1) * 128)
            r0 = t0 + tt * 128
            for k_, dsts in ((0, LWP), (1, APR)):
                for n in range(2):
                    y, ty = y_r.next()
                    ns = slice(n * 64, (n + 1) * 64)
                    for nh in range(2):
                        ps, tps = self.ps()
                        self.mm(ps, hT[k_][0][ns, ts_], l2[k_][0][ns, nh * 512:(nh + 1) * 512], True, True,
                                [hT[k_][1], l2[k_][1]], [tps])
                        self.copy(self.ev(), y[:, nh * 512:(nh + 1) * 512], ps, [tps], [ty])
                    self.dma(dsts[n][r0:r0 + 128, :], y, reads=[ty])
            y, ty = y_r.next()
            for nh in range(2):
                ps, tps = self.ps()
                self.mm(ps, hT[2][0][:, ts_], l2[2][0][:, nh * 512:(nh + 1) * 512], True, True, [hT[2][1], l2[2][1]], [tps])
                self.copy(self.ev(), y[:, nh * 512:(nh + 1) * 512], ps, [tps], [ty])
            self.dma(GG[r0:r0 + 128, :], y, reads=[ty])


Prog.stage_rwkv_proj = stage_rwkv_proj


def stage_rwkv_pass(self, n, RR, KK, VV, LWP, APR, GG, OF, BON, MIX):
    self.stage_begin()
    w = self.w
    C = self.cst
    tc_ = self.t_consts
    ident = self.ident
    ones = self.ones
    cf = C("ui") if n == 0 else C("li")
    o_us = CONST_LAYOUT["us"][0]
    o_ls = CONST_LAYOUT["ls"][0]
    m2 = self.consts[:, o_us:o_us + 256] if n == 0 else self.consts[:, o_ls:o_ls + 256]
    m1 = C("ls") if n == 0 else C("us")
    def rowp(ap):
        t_, tk_ = self.tile([128, D])
        self.bcast_row(t_, tk_, ap, D)
        return t_, tk_
    w0, tw0 = rowp(w["od_w0"][n])
    a0, ta0 = rowp(w["od_a0"][n])
    k_k, tkk_ = rowp(w["od_k_k"])
    k_a, tka = rowp(w["od_k_a"])
    r_k, trk = rowp(w["od_r_k"])
    if n == 1:
        lnw, tlnw = rowp(w["od_lnx_w"])
        lnb, tlnb = rowp(w["od_lnx_b"])
    mk = lambda: self.tile([128, D])
    r, tr_ = mk()
    k, tk = mk()
    v, tv = mk()
    lw, tlw = mk()
    a, ta = mk()
    kk, tkk = mk()
    kd, tkd = mk()
    b, tb = mk()
    t1, tt1 = mk()
    t2, tt2 = mk()
    E, tE = mk()
    Rt, tRt = mk()
    KKt, tKKt = mk()
    nBt, tnBt = mk()
    Kt, tKt = mk()
    nBh, tnBh = mk()
    Kh, tKh = mk()
    O, tO = mk()
    st, tst = self.tile([128, 64])
    PL, tPL = self.tile([128, 8])
    nBtT, tnBtT = self.tile([128, 8, 128])
    KtT, tKtT = self.tile([128, 8, 128])
    KR, tKR = self.tile([128, 8, 2, 128])
    ST = [self.tile([128, 64]) for _ in range(8)]
    for s_, ts_ in ST:
        self.memset(DVE, s_, 0.0, [ts_])
    NS = 4
    slots = []
    for s in range(NS):
        d_ = {}
        for nm, shp in (("AB", [128, 256]), ("AK", [128, 256]), ("N0T", [128, 128]), ("TA", [128, 128]), ("TB", [128, 128]),
                        ("NA", [128, 128]), ("NAT", [128, 128]), ("NB", [128, 128]), ("NBT", [128, 128]),
                        ("W1", [128, 64]), ("MU", [128, 128]), ("Q1T", [128, 128]), ("GPT", [128, 64])):
            d_[nm] = self.tile(shp)
        slots.append(d_)

    def head_gen(h, sl):
        c = h // 2
        P = slice((h % 2) * 64, (h % 2) * 64 + 64)
        cb = slice(h * 64, (h + 1) * 64)
        AB, tAB = sl["AB"]
        AK, tAK = sl["AK"]
        N0T, tN0T = sl["N0T"]
        W1, tW1 = sl["W1"]
        MU, tMU = sl["MU"]
        Q1T, tQ1T = sl["Q1T"]
        GPT, tGPT = sl["GPT"]
        Sx, tSx = ST[c]
        krf = KR[P, c, :, :].rearrange("p a t -> p (a t)")
        p1, tp1 = self.ps()
        self.mm(p1[:, 0:256], nBtT[P, c, :], krf, True, True, [tnBtT, tKR], [tp1])
        self.tt(DVE, AB, p1[:, 0:256], m2, ALU.mult, [tp1, tc_], [tAB])
        yield
        p2, tp2 = self.ps()
        self.mm(p2[:, 0:256], KtT[P, c, :], krf, True, True, [tKtT, tKR], [tp2])
        self.tt(DVE, AK, p2[:, 0:256], m2, ALU.mult, [tp2, tc_], [tAK])
        yield
        p3, tp3 = self.ps()
        self.mm(p3[:, 0:128], KR[P, c, 0, :], nBtT[P, c, :], True, True, [tKR, tnBtT], [tp3])
        self.tt(DVE, N0T, p3[:, 0:128], m1, ALU.mult, [tp3, tc_], [tN0T])
        yield
        Tc, tTc = sl["TA"]
        To, tTo = sl["TB"]
        self.tt(POOL, Tc, AB[:, 0:128], ident, ALU.add, [tAB, tc_], [tTc])
        Na, tNa = AB[:, 0:128], tAB
        NaT, tNaT = N0T, tN0T
        bufs = [(sl["NA"], sl["NAT"]), (sl["NB"], sl["NBT"])]
        for lev in range(1, 7):
            (Nb, tNb), (NbT, tNbT) = bufs[lev % 2]
            if lev < 6:
                pn, tpn = self.ps()
                self.mm(pn[:, 0:128], NaT, Na, True, True, [tNaT, tNa], [tpn])
                self.copy(ACT, Nb, pn[:, 0:128], [tpn], [tNb])
            pnt, tpnt = self.ps()
            self.mm(pnt[:, 0:128], Na, NaT, True, True, [tNa, tNaT], [tpnt])
            self.copy(ACT, NbT, pnt[:, 0:128], [tpnt], [tNbT])
            yield
            pt_, tpt_ = self.ps()
            self.mm(pt_[:, 0:128], NbT, Tc, True, True, [tNbT, tTc], [tpt_])
            self.tt(DVE, To, pt_[:, 0:128], Tc, ALU.add, [tpt_, tTc], [tTo])
            (Tc, tTc), (To, tTo) = (To, tTo), (Tc, tTc)
            Na, tNa, NaT, tNaT = Nb, tNb, NbT, tNbT
            yield
        BrbT = AB[:, 128:256]
        AakT = AK[:, 0:128]
        BrkT = AK[:, 128:256]
        pw, tpw = self.ps()
        self.mm(pw[:, 0:64], AakT, v[:, cb], True, True, [tAK, tv], [tpw])
        self.copy(ACT, W1, pw[:, 0:64], [tpw], [tW1])
        yield
        pm, tpm = self.ps()
        self.mm(pm[:, 0:64], Tc, KKt[:, cb], True, True, [tTc, tKKt], [tpm])
        self.mm(pm[:, 64:128], Tc, W1, True, True, [tTc, tW1], [tpm])
        self.copy(ACT, MU, pm[:, 0:128], [tpm], [tMU])
        yield
        pq, tpq = self.ps()
        self.mm(pq[P, 0:128], MU[:, 0:64], BrbT, True, True, [tMU, tAB], [tpq])
        self.tt(DVE, Q1T[P, :], pq[P, 0:128], KR[P, c, 1, :], ALU.add, [tpq, tKR], [tQ1T])
        pg, tpg = self.ps()
        self.mm(pg[P, 0:64], MU[:, 0:64], nBh[:, cb], True, True, [tMU, tnBh], [tpg])
        self.copy(ACT, GPT[P, :], pg[P, 0:64], [tpg], [tGPT])
        yield
        py, tpy = self.ps()
        self.mm(py[:, 0:64], BrbT, MU[:, 64:128], True, False, [tAB, tMU], [tpy])
        self.mm(py[:, 0:64], BrkT, v[:, cb], False, False, [tAK, tv], [tpy])
        self.mm(py[:, 0:64], Q1T[P, :], Sx[P, :], False, True, [tQ1T, tSx], [tpy])
        self.copy(ACT, O[:, cb], py[:, 0:64], [tpy], [tO])
        ph, tph = self.ps()
        self.mm(ph[P, 0:64], nBh[:, cb], MU[:, 64:128], True, False, [tnBh, tMU], [tph])
        self.mm(ph[P, 0:64], Kh[:, cb], v[:, cb], False, False, [tKh, tv], [tph])
        self.mm(ph[P, 0:64], GPT[P, :], Sx[P, :], False, True, [tGPT, tSx], [tph])
        self.stt(Sx[P, :], Sx[P, :], PL[P, c:c + 1], ph[P, 0:64], ALU.mult, ALU.add, [tSx, tPL, tph], [tSx])
        yield

    h16 = lambda ap: ap.rearrange("p (h m) -> p h m", h=16)
    b16 = lambda ap: ap.rearrange("p (h o) -> p h o", o=1).to_broadcast([128, 16, 64])
    order = list(range(NTILE)) if n == 0 else list(reversed(range(NTILE)))
    if DBG_TILES:
        order = order[:DBG_TILES]
    for i in order:
        u = i // 16
        rows = slice(i * 128, (i + 1) * 128)
        if (n == 0 and i % 16 == 0) or (n == 1 and i % 16 == 15):
            fc = u if n == 0 else 3 + u
            for s_, ts_ in ST:
                self.ts(DVE, s_, s_, self.flags[:, fc:fc + 1], ALU.mult, [ts_, self.t_flags], [ts_])
        self.dma(r, RR[rows, :], writes=[tr_])
        self.dma(k, KK[rows, :], writes=[tk])
        self.dma(v, VV[rows, :], writes=[tv])
        self.dma(lw, LWP[n][rows, :], writes=[tlw])
        self.dma(a, APR[n][rows, :], writes=[ta])
        self.tt(POOL, lw, lw, w0, ALU.add, [tlw, tw0], [tlw])
        self.act(lw, lw, AF.Sigmoid, [tlw], [tlw])
        self.ts(DVE, lw, lw, -0.6065306597126334, ALU.mult, [tlw], [tlw])
        self.tt(POOL, a, a, a0, ALU.add, [ta, ta0], [ta])
        self.act(a, a, AF.Sigmoid, [ta], [ta])
        self.tt(POOL, t1, k, k_k, ALU.mult, [tk, tkk_], [tt1])
        self.tt(POOL, t2, t1, t1, ALU.mult, [tt1], [tt2])
        self.red(DVE, st[:, 0:16], h16(t2), ALU.add, [tt2], [tst])
        self.act(st[:, 0:16], st[:, 0:16], AF.Sqrt, [tst], [tst])
        self.ts(DVE, st[:, 0:16], st[:, 0:16], 1e-12, ALU.max, [tst], [tst])
        self.recip(st[:, 16:32], st[:, 0:16], [tst], [tst])
        self.tt(DVE, h16(kk), h16(t1), b16(st[:, 16:32]), ALU.mult, [tt1, tst], [tkk])
        self.stt(t1, a, -1.0, k_a, ALU.add, ALU.mult, [ta, tka], [tt1])
        self.tt(POOL, t1, t1, k, ALU.mult, [tt1, tk], [tt1])
        self.tt(POOL, kd, t1, k, ALU.add, [tt1, tk], [tkd])
        self.tt(POOL, b, kk, a, ALU.mult, [tkk, ta], [tb])
        self.tt(POOL, t2, r, kd, ALU.mult, [tr_, tkd], [tt2])
        self.tt(POOL, t2, t2, r_k, ALU.mult, [tt2, trk], [tt2])
        self.red(DVE, st[:, 32:48], h16(t2), ALU.add, [tt2], [tst])
        self.tt(DVE, h16(t2), h16(v), b16(st[:, 32:48]), ALU.mult, [tv, tst], [tt2])
        if n == 0:
            self.dma(BON[rows, :], t2, reads=[tt2])
        pcs = []
        pts = []
        for nh in range(2):
            pc, tpc = self.ps()
            self.mm(pc, cf, lw[:, nh * 512:(nh + 1) * 512], True, True, [tc_, tlw], [tpc])
            pcs.append((pc, tpc))
        for nh in range(2):
            pt, tpt = self.ps()
            self.mm(pt, ones, lw[:, nh * 512:(nh + 1) * 512], True, True, [tc_, tlw], [tpt])
            pts.append((pt, tpt))
        ppl, tppl = self.ps()
        for c in range(8):
            self.mm(ppl[:, c:c + 1], lw[:, c * 128:(c + 1) * 128], ones[:, 0:1], True, True, [tlw, tc_], [tppl])
        self.act(PL, ppl[:, 0:8], AF.Exp, [tppl], [tPL])
        for nh in range(2):
            hs = slice(nh * 512, (nh + 1) * 512)
            pc, tpc = pcs[nh]
            self.act(E[:, hs], pc, AF.Exp, [tpc], [tE])
        self.tt(DVE, Rt, r, E, ALU.mult, [tr_, tE], [tRt])
        for nh in range(2):
            hs = slice(nh * 512, (nh + 1) * 512)
            pc, tpc = pcs[nh]
            self.tt(DVE, t1[:, hs], pc, lw[:, hs], ALU.subtract, [tpc, tlw], [tt1])
        self.act(t1, t1, AF.Exp, [tt1], [tt1])
        self.tt(POOL, KKt, kk, t1, ALU.mult, [tkk, tt1], [tKKt])
        for nh in range(2):
            hs = slice(nh * 512, (nh + 1) * 512)
            pc, tpc = pcs[nh]
            self.act(E[:, hs], pc, AF.Exp, [tpc], [tE], scale=-1.0)
        self.stt(nBt, b, -1.0, E, ALU.mult, ALU.mult, [tb, tE], [tnBt])
        self.tt(POOL, Kt, kd, E, ALU.mult, [tkd, tE], [tKt])
        for nh in range(2):
            hs = slice(nh * 512, (nh + 1) * 512)
            pt, tpt = pts[nh]
            self.act(t1[:, hs], pt, AF.Exp, [tpt], [tt1])
        self.tt(POOL, nBh, nBt, t1, ALU.mult, [tnBt, tt1], [tnBh])
        self.tt(POOL, Kh, Kt, t1, ALU.mult, [tKt, tt1], [tKh])
        for (src, tsrc, dst, tdst) in ((nBt, tnBt, nBtT, tnBtT), (Kt, tKt, KtT, tKtT)):
            for g in range(2):
                ps, tps = self.ps()
                for cc in range(4):
                    c = g * 4 + cc
                    self.tr(ps[:, cc * 128:(cc + 1) * 128], src[:, c * 128:(c + 1) * 128], [tsrc], [tps])
                self.copy(self.ev(), dst[:, g * 4:(g + 1) * 4, :], ps.rearrange("p (c t) -> p c t", c=4), [tps], [tdst])
        for g in range(4):
            ps, tps = self.ps()
            for cc in range(2):
                c = g * 2 + cc
                self.tr(ps[:, cc * 256:cc * 256 + 128], KKt[:, c * 128:(c + 1) * 128], [tKKt], [tps])
                self.tr(ps[:, cc * 256 + 128:cc * 256 + 256], Rt[:, c * 128:(c + 1) * 128], [tRt], [tps])
            self.copy(self.ev(), KR[:, g * 2:(g + 1) * 2, :, :], ps.rearrange("p (c a t) -> p c a t", c=2, a=2), [tps], [tKR])
        for hg in range(0, 16, NS):
            gens = [head_gen(hg + s, slots[s]) for s in range(NS)]
            alive = list(gens)
            while alive:
                nxt = []
                for g_ in alive:
                    try:
                        next(g_)
                        nxt.append(g_)
                    except StopIteration:
                        pass
                alive = nxt
        if n == 0:
            self.dma(OF[rows, :], O, reads=[tO])
        else:
            self.dma(t1, OF[rows, :], writes=[tt1])
            self.tt(DVE, O, O, t1, ALU.add, [tO, tt1], [tO])
            self.dma(Rt, BON[rows, :], writes=[tRt])
            self.dma(KKt, GG[rows, :], writes=[tKKt])
            self.red(DVE, st[:, 0:16], h16(O), ALU.add, [tO], [tst])
            self.ts(DVE, st[:, 0:16], st[:, 0:16], 1.0 / 64.0, ALU.mult, [tst], [tst])
            self.tt(DVE, h16(O), h16(O), b16(st[:, 0:16]), ALU.subtract, [tO, tst], [tO])
            self.tt(POOL, t1, O, O, ALU.mult, [tO], [tt1])
            self.red(DVE, st[:, 16:32], h16(t1), ALU.add, [tt1], [tst])
            self.ts(DVE, st[:, 16:32], st[:, 16:32], 1.0 / 64.0, ALU.mult, [tst], [tst], s2=64e-5, op1=ALU.add)
            self.act(st[:, 16:32], st[:, 16:32], AF.Sqrt, [tst], [tst])
            self.recip(st[:, 48:64], st[:, 16:32], [tst], [tst])
            self.tt(DVE, h16(O), h16(O), b16(st[:, 48:64]), ALU.mult, [tO, tst], [tO])
            self.tt(POOL, O, O, lnw, ALU.mult, [tO, tlnw], [tO])
            self.tt(POOL, O, O, lnb, ALU.add, [tO, tlnb], [tO])
            self.tt(POOL, O, O, t2, ALU.add, [tO, tt2], [tO])
            self.tt(POOL, O, O, Rt, ALU.add, [tO, tRt], [tO])
            self.tt(DVE, O, O, KKt, ALU.mult, [tO, tKKt], [tO])
            self.dma(MIX[rows, :], O, reads=[tO])


Prog.stage_rwkv_pass = stage_rwkv_pass


def odd_mixer(self):
    mkd = lambda nm: self.dr(nm, [T, D])
    RR, KK, VV, GG = mkd("RR"), mkd("KK"), mkd("VV"), mkd("GG")
    LWP = [mkd("LWP0"), mkd("LWP1")]
    APR = [mkd("APR0"), mkd("APR1")]
    OF, BON, MIX = mkd("OF"), mkd("BON"), mkd("MIX")
    self.stage_rwkv_proj(RR, KK, VV, LWP, APR, GG)
    self.stage_rwkv_pass(0, RR, KK, VV, LWP, APR, GG, OF, BON, MIX)
    self.stage_rwkv_pass(1, RR, KK, VV, LWP, APR, GG, OF, BON, MIX)
    XT2 = self.dr("XT2", [D, T], BF16)
    self.stage_ln(MIX, None, None, do_ln=False, xt_out=XT2)
    self.stage_linear(self.w["od_w_out"], self.H, D, xt=XT2)


Prog.odd_mixer = odd_mixer


def odd_weights(inp, j):
    return {
        "od_mu_t": np.ascontiguousarray(inp["od_mu"][j].reshape(6, 8, 128).transpose(2, 0, 1).reshape(128, 48)),
        "od_w_rkv": inp["od_w_rkv"][j],
        "od_w1c": np.ascontiguousarray(inp["od_w1"][j].transpose(1, 0, 2).reshape(D, 128)),
        "od_a1c": np.ascontiguousarray(inp["od_a1"][j].transpose(1, 0, 2).reshape(D, 128)),
        "od_g1": inp["od_g1"][j],
        "od_w2c": np.ascontiguousarray(inp["od_w2"][j].reshape(128, D)),
        "od_a2c": np.ascontiguousarray(inp["od_a2"][j].reshape(128, D)),
        "od_g2": inp["od_g2"][j],
        "od_w0": inp["od_w0"][j], "od_a0": inp["od_a0"][j], "od_k_k": inp["od_k_k"][j], "od_k_a": inp["od_k_a"][j],
        "od_r_k": inp["od_r_k"][j], "od_lnx_w": inp["od_lnx_w"][j], "od_lnx_b": inp["od_lnx_b"][j],
        "od_w_out": inp["od_w_out"][j],
    }


ODD_SHAPES = {"od_mu_t": [128, 48], "od_w_rkv": [3, D, D], "od_w1c": [D, 128], "od_a1c": [D, 128], "od_g1": [D, 128],
              "od_w2c": [128, D], "od_a2c": [128, D], "od_g2": [128, D], "od_w0": [2, D], "od_a0": [2, D],
              "od_k_k": [D], "od_k_a": [D], "od_r_k": [D], "od_lnx_w": [D], "od_lnx_b": [D], "od_w_out": [D, D]}


EVEN_SHAPES = {"ev_w_in": [D, 4608], "ev_lb_logits": [2, 2, 512], "ev_norm_a": [512], "ev_norm_b": [512],
               "ev_w_out": [D, D], "rope": [T, 192]}
COMMON_MIX = {"x": [T, D], "mem": [NU * MEM, D], "consts": [128, NCONST], "flags": [128, 16], "ln_w": [3, D], "ln_b": [3, D],
              "ca_w_q": [D, D], "ca_w_kv": [D, 2 * D], "ca_w_out": [D, D], "moe_router_t": [NE, D]}
MOE_INS = {"x": [T, D], "aff_loc": [T, NE], "aff_all": [NCORES * T, NE], "consts": [128, NCONST], "flags": [128, 16],
           "moe_w_in": [NE, D, 2 * DEXP], "moe_w_out": [NE, DEXP, D], "ln_w": [3, D], "ln_b": [3, D]}


def build_prog(has_moe, kind, j):
    ins = {"x": [T, D], "consts": [128, NCONST], "flags": [128, 16]}
    outs = {}
    if has_moe:
        ins.update({"aff_loc": [T, NE], "aff_all": [NCORES * T, NE], "moe_w_in": [NE, D, 2 * DEXP],
                    "moe_w_out": [NE, DEXP, D], "mln_w": [3, D], "mln_b": [3, D]})
    if kind is not None:
        ins.update({k: v for k, v in COMMON_MIX.items() if k not in ins})
        ins.update(EVEN_SHAPES if kind == "even" else ODD_SHAPES)
        outs.update({"x2": [T, D], "aff": [T, NE]})
    else:
        outs["x3"] = [T, D]
    P = Prog(ins, outs)
    xin = P.i["x"]
    P.stage_ln(xin, None, None, do_ln=False)
    if has_moe:
        P.stage_topk()
        P.stage_moe()
        dst = P.o["x3"] if kind is None else P.X[1]
        P.stage_ln(xin, P.H, dst, lnw=P.i["mln_w"][2], lnb=P.i["mln_b"][2])
        xin = dst
    if kind is None:
        return P
    if kind == "even":
        P.even_mixer(j)
    else:
        P.odd_mixer()
    P.stage_ln(xin, P.H, P.X[0], lnw=P.i["ln_w"][0], lnb=P.i["ln_b"][0])
    P.stage_ln(P.i["mem"], None, None, do_ln=False, xt_out=P.MT, ntok=NU * MEM)
    P.stage_xattn()
    P.stage_ln(P.X[0], P.H, P.o["x2"], lnw=P.i["ln_w"][1], lnb=P.i["ln_b"][1], router_w=P.i["moe_router_t"])
    return P


def split_cores(inp):
    xs, ms = [], []
    for c in range(NCORES):
        xl, ml = [], []
        for (g, b, half) in core_units(c):
            if g == "p":
                xl.append(inp["x_prompt"][b, half * TU:(half + 1) * TU])
                ml.append(inp["mem_prompt"][b])
            else:
                xl.append(inp["x_sample"][b])
                ml.append(inp["mem_sample"][b])
        xs.append(np.ascontiguousarray(np.concatenate(xl, 0)))
        ms.append(np.ascontiguousarray(np.concatenate(ml, 0)))
    return xs, ms


def run_launch(moe_layer, mix_layer, inp, xs, ms, affs):
    kind = None if mix_layer is None else ("even" if mix_layer % 2 == 0 else "odd")
    j = 0 if mix_layer is None else mix_layer // 2
    P = build_prog(moe_layer is not None, kind, j)
    nc = P.finish()
    common = {"consts": make_consts()}
    if moe_layer is not None:
        common.update({"aff_all": np.ascontiguousarray(np.concatenate(affs, 0)), "moe_w_in": inp["moe_w_in"][moe_layer],
                       "moe_w_out": inp["moe_w_out"][moe_layer], "mln_w": inp["ln_w"][moe_layer],
                       "mln_b": inp["ln_b"][moe_layer]})
    if kind is not None:
        L = mix_layer
        common.update({"ln_w": inp["ln_w"][L], "ln_b": inp["ln_b"][L], "ca_w_q": inp["ca_w_q"][L],
                       "ca_w_kv": inp["ca_w_kv"][L], "ca_w_out": inp["ca_w_out"][L],
                       "moe_router_t": np.ascontiguousarray(inp["moe_router"][L].T)})
        if kind == "even":
            common.update({"ev_w_in": inp["ev_w_in"][j], "ev_lb_logits": inp["ev_lb_logits"],
                           "ev_norm_a": inp["ev_norm_a"][j], "ev_norm_b": inp["ev_norm_b"][j],
                           "ev_w_out": inp["ev_w_out"][j]})
        else:
            common.update(odd_weights(inp, j))
    maps = []
    for c in range(NCORES):
        m = dict(common)
        m["x"] = xs[c]
        m["flags"] = make_flags(c)
        if moe_layer is not None:
            m["aff_loc"] = affs[c]
        if kind is not None:
            m["mem"] = ms[c]
        if kind == "even":
            m["rope"] = make_rope(c)
        maps.append(m)
    res = run_bass_kernel_spmd(nc, maps, core_ids=list(range(NCORES)))
    if kind is None:
        return [r["x3"] for r in res.results], None
    return [r["x2"] for r in res.results], [r["aff"] for r in res.results]


def kernel(**inp):
    inp = {k: np.asarray(v) for k, v in inp.items()}
    xs, ms = split_cores(inp)
    affs = None
    for step in range(DEPTH + 1):
        moe_layer = step - 1 if step > 0 else None
        mix_layer = step if step < DEPTH else None
        xs, affs = run_launch(moe_layer, mix_layer, inp, xs, ms, affs)
    yp = np.zeros((4, 4096, D), np.float32)
    ysm = np.zeros((16, 2048, D), np.float32)
    for c in range(NCORES):
        for u, (g, b, half) in enumerate(core_units(c)):
            blk = xs[c][u * TU:(u + 1) * TU]
            if g == "p":
                yp[b, half * TU:(half + 1) * TU] = blk
            else:
                ysm[b] = blk
    return (yp, ysm)
```
